# Optimizing a Trainium2 kernel written in Bass

```python
import jax, jax.numpy as jnp
from jax import lax
import numpy as np

D_MODEL = 1024
BATCH = 4
SEQ = 8192
DEPTH = 2

MIX_WIDTH = D_MODEL
BRANCH = MIX_WIDTH // 2
GLA_HEADS = 4
GLA_DK = BRANCH // 2 // GLA_HEADS
GLA_DV = BRANCH // GLA_HEADS
GLA_RANK = 16
GLA_TAU = 16.0
GLA_CHUNK = 64
SGU_GROUPS = 4
SGU_CHUNK = 128
SGU_CH = BRANCH // SGU_GROUPS
RET_HEADS = 4
RET_DK = BRANCH // 2 // RET_HEADS
RET_DV = BRANCH // RET_HEADS
RET_CHUNK = 128
ROPE_BASE = 10000.0
POOL_WINDOWS = (2, 4, 8, 16)
POOL_CH = BRANCH // len(POOL_WINDOWS)
DN_ALPHA = (2 * DEPTH) ** 0.25
DN_BETA = (8 * DEPTH) ** -0.25
LN_EPS = 1e-5

EVEN_SIZES = (GLA_HEADS * GLA_DK, GLA_HEADS * GLA_DK, BRANCH, BRANCH, GLA_RANK,
              BRANCH, BRANCH, BRANCH)
ODD_SIZES = (RET_HEADS * RET_DK, RET_HEADS * RET_DK, BRANCH, BRANCH,
             BRANCH, BRANCH)

kernel_name = "hybrid_gla_sgu_retention_pool_deepnorm"


def _split(h, sizes):
    idx = [int(i) for i in np.cumsum(sizes)[:-1]]
    return jnp.split(h, idx, axis=-1)


def _layer_norm(x, g, b):
    xf = x.astype(jnp.float32)
    mu = jnp.mean(xf, axis=-1, keepdims=True)
    var = jnp.mean(jnp.square(xf - mu), axis=-1, keepdims=True)
    return ((xf - mu) * lax.rsqrt(var + LN_EPS) * g + b).astype(x.dtype)


def _head_rmsnorm(o, g):
    of = o.astype(jnp.float32)
    return of * lax.rsqrt(jnp.mean(jnp.square(of), axis=-1, keepdims=True) + LN_EPS) * g


def _head_groupnorm(o, g):
    of = o.astype(jnp.float32)
    mu = jnp.mean(of, axis=-1, keepdims=True)
    var = jnp.mean(jnp.square(of - mu), axis=-1, keepdims=True)
    return (of - mu) * lax.rsqrt(var + LN_EPS) * g


def _chunk_scan(decay, chunk_state):
    def step(s_prev, inp):
        dec, cs = inp
        return dec * s_prev + cs, s_prev
    init = jnp.zeros(chunk_state.shape[1:], jnp.float32)
    _, s_in = lax.scan(step, init, (decay, chunk_state))
    return s_in


def _gla(q, k, v, log_a):
    B, S, H, dk = q.shape
    dv = v.shape[-1]
    C = GLA_CHUNK
    n = S // C
    f32 = jnp.float32
    q = q.astype(f32).reshape(B, n, C, H, dk) * (dk ** -0.5)
    k = k.astype(f32).reshape(B, n, C, H, dk)
    v = v.astype(f32).reshape(B, n, C, H, dv)
    b = jnp.cumsum(log_a.astype(f32).reshape(B, n, C, H, dk), axis=2)
    q_dec = q * jnp.exp(b)
    k_dec = k * jnp.exp(-b)
    causal = jnp.tril(jnp.ones((C, C), dtype=bool))
    scores = jnp.einsum('bnihd,bnjhd->bnhij', q_dec, k_dec)
    scores = jnp.where(causal, scores, 0.0)
    o_intra = jnp.einsum('bnhij,bnjhe->bnihe', scores, v)
    b_last = b[:, :, -1]
    k_state = k * jnp.exp(b_last[:, :, None] - b)
    chunk_state = jnp.einsum('bnjhd,bnjhe->bnhde', k_state, v)
    decay = jnp.exp(b_last)[..., None]
    s_in = _chunk_scan(jnp.moveaxis(decay, 1, 0), jnp.moveaxis(chunk_state, 1, 0))
    s_in = jnp.moveaxis(s_in, 0, 1)
    o_inter = jnp.einsum('bnihd,bnhde->bnihe', q_dec, s_in)
    return (o_intra + o_inter).reshape(B, S, H, dv)


def _spatial_gating(u, sv, ln_g, ln_b, w_s, b_s):
    B, S, _ = u.shape
    n = S // SGU_CHUNK
    svr = sv.reshape(B, n, SGU_CHUNK, SGU_GROUPS, SGU_CH)
    svr = _layer_norm(svr, ln_g, ln_b)
    w = jnp.where(jnp.tril(jnp.ones((SGU_CHUNK, SGU_CHUNK), dtype=bool)), w_s, 0.0)
    s = jnp.einsum('gts,bnsgc->bntgc', w, svr) + jnp.transpose(b_s)[None, None, :, :, None]
    return u * s.reshape(B, S, BRANCH).astype(u.dtype)


def _rotary(x, positions):
    half = x.shape[-1] // 2
    inv = ROPE_BASE ** (-jnp.arange(half, dtype=jnp.float32) / half)
    ang = positions.astype(jnp.float32)[..., None] * inv
    cos = jnp.cos(ang)[:, :, None, :]
    sin = jnp.sin(ang)[:, :, None, :]
    xf = x.astype(jnp.float32)
    x1, x2 = xf[..., :half], xf[..., half:]
    return jnp.concatenate([x1 * cos - x2 * sin, x1 * sin + x2 * cos], axis=-1)


def _retention(q, k, v):
    B, S, H, dk = q.shape
    dv = v.shape[-1]
    C = RET_CHUNK
    n = S // C
    f32 = jnp.float32
    log_gamma = jnp.log(1.0 - 2.0 ** (-5.0 - jnp.arange(H, dtype=f32)))
    q = q.astype(f32).reshape(B, n, C, H, dk) * (dk ** -0.5)
    k = k.astype(f32).reshape(B, n, C, H, dk)
    v = v.astype(f32).reshape(B, n, C, H, dv)
    idx = jnp.arange(C, dtype=f32)
    rel = idx[:, None] - idx[None, :]
    dmat = jnp.where(rel >= 0, jnp.exp(jnp.maximum(rel, 0.0)[None] * log_gamma[:, None, None]), 0.0)
    scores = jnp.einsum('bnihd,bnjhd->bnhij', q, k) * dmat
    o_intra = jnp.einsum('bnhij,bnjhe->bnihe', scores, v)
    w_state = jnp.exp((C - 1.0 - idx)[None] * log_gamma[:, None])
    chunk_state = jnp.einsum('bnjhd,hj,bnjhe->bnhde', k, w_state, v)
    chunk_decay = jnp.exp(C * log_gamma)[None, None, :, None, None]
    chunk_decay = jnp.broadcast_to(chunk_decay, (B, n, H, 1, 1))
    s_in = _chunk_scan(jnp.moveaxis(chunk_decay, 1, 0), jnp.moveaxis(chunk_state, 1, 0))
    s_in = jnp.moveaxis(s_in, 0, 1)
    w_inter = jnp.exp((idx + 1.0)[None] * log_gamma[:, None])
    o_inter = jnp.einsum('bnihd,hi,bnhde->bnihe', q, w_inter, s_in)
    return (o_intra + o_inter).reshape(B, S, H, dv)


def _multiscale_pool(p, w_pool, scale):
    B, S, _ = p.shape
    pf = p.astype(jnp.float32)
    cs = jnp.concatenate([jnp.zeros((B, 1, BRANCH), jnp.float32), jnp.cumsum(pf, axis=1)], axis=1)
    t = jnp.arange(S)
    outs = []
    for g, w in enumerate(POOL_WINDOWS):
        lo, hi = g * POOL_CH, (g + 1) * POOL_CH
        start = jnp.maximum(t + 1 - w, 0)
        win_sum = cs[:, 1:, lo:hi] - cs[:, start, lo:hi]
        cnt = jnp.minimum(t + 1, w).astype(jnp.float32)[None, :, None]
        outs.append(win_sum / cnt - pf[..., lo:hi])
    pooled = jnp.concatenate(outs, axis=-1).reshape(B, S, len(POOL_WINDOWS), POOL_CH)
    y = jnp.einsum('bsgc,gcd->bsgd', pooled, w_pool).reshape(B, S, BRANCH) * scale
    return y.astype(p.dtype)


def _even_layer(x, w_in, w_a2, b_a, gla_norm_g, sgu_ln_g, sgu_ln_b, w_s, b_s, w_out, ln_g, ln_b):
    B, S, _ = x.shape
    h = x @ w_in
    q, k, v, g_a, a_lr, u, sv, g_b = _split(h, EVEN_SIZES)
    log_a = jax.nn.log_sigmoid((a_lr @ w_a2 + b_a).astype(jnp.float32)) / GLA_TAU
    o = _gla(q.reshape(B, S, GLA_HEADS, GLA_DK), k.reshape(B, S, GLA_HEADS, GLA_DK),
             v.reshape(B, S, GLA_HEADS, GLA_DV), log_a.reshape(B, S, GLA_HEADS, GLA_DK))
    o_a = _head_rmsnorm(o, gla_norm_g).reshape(B, S, BRANCH).astype(x.dtype) * jax.nn.silu(g_a)
    o_b = _spatial_gating(jax.nn.gelu(u), jax.nn.gelu(sv), sgu_ln_g, sgu_ln_b, w_s, b_s) * jax.nn.silu(g_b)
    y = jnp.concatenate([o_a, o_b], axis=-1) @ w_out
    return _layer_norm(DN_ALPHA * x + y, ln_g, ln_b)


def _odd_layer(x, positions, w_in, ret_norm_g, w_pool, pool_scale, w_out, ln_g, ln_b):
    B, S, _ = x.shape
    h = x @ w_in
    q, k, v, g_c, p, g_d = _split(h, ODD_SIZES)
    qr = _rotary(q.reshape(B, S, RET_HEADS, RET_DK), positions)
    kr = _rotary(k.reshape(B, S, RET_HEADS, RET_DK), positions)
    o = _retention(qr, kr, v.reshape(B, S, RET_HEADS, RET_DV))
    o_c = _head_groupnorm(o, ret_norm_g).reshape(B, S, BRANCH).astype(x.dtype) * jax.nn.silu(g_c)
    o_d = _multiscale_pool(p, w_pool, pool_scale) * jax.nn.silu(g_d)
    y = jnp.concatenate([o_c, o_d], axis=-1) @ w_out
    return _layer_norm(DN_ALPHA * x + y, ln_g, ln_b)


def setup_inputs(seed: int = 0) -> dict:
    key = jax.random.key(seed)
    ks = jax.random.split(key, 24)
    f32 = jnp.float32
    nrm = lambda k, shape, s: jax.random.normal(k, shape, f32) * s
    d_even = sum(EVEN_SIZES)
    d_odd = sum(ODD_SIZES)
    return {
        "x": jax.random.normal(ks[0], (BATCH, SEQ, D_MODEL), f32),
        "positions": jnp.broadcast_to(jnp.arange(SEQ, dtype=jnp.int32), (BATCH, SEQ)),
        "l0_w_in": nrm(ks[1], (D_MODEL, d_even), D_MODEL ** -0.5),
        "l0_w_a2": nrm(ks[2], (GLA_RANK, GLA_HEADS * GLA_DK), GLA_RANK ** -0.5),
        "l0_b_a": nrm(ks[3], (GLA_HEADS * GLA_DK,), 0.1),
        "l0_gla_norm_g": 1.0 + nrm(ks[4], (GLA_HEADS, GLA_DV), 0.1),
        "l0_sgu_ln_g": 1.0 + nrm(ks[5], (SGU_GROUPS, SGU_CH), 0.1),
        "l0_sgu_ln_b": nrm(ks[6], (SGU_GROUPS, SGU_CH), 0.02),
        "l0_w_s": nrm(ks[7], (SGU_GROUPS, SGU_CHUNK, SGU_CHUNK), 0.5 * SGU_CHUNK ** -0.5),
        "l0_b_s": 1.0 + nrm(ks[8], (SGU_GROUPS, SGU_CHUNK), 0.1),
        "l0_w_out": nrm(ks[9], (MIX_WIDTH, D_MODEL), DN_BETA * MIX_WIDTH ** -0.5),
        "l0_ln_g": 1.0 + nrm(ks[10], (D_MODEL,), 0.1),
        "l0_ln_b": nrm(ks[11], (D_MODEL,), 0.02),
        "l1_w_in": nrm(ks[12], (D_MODEL, d_odd), D_MODEL ** -0.5),
        "l1_ret_norm_g": 1.0 + nrm(ks[13], (RET_HEADS, RET_DV), 0.1),
        "l1_w_pool": nrm(ks[14], (len(POOL_WINDOWS), POOL_CH, POOL_CH), POOL_CH ** -0.5),
        "l1_pool_scale": 0.5 + nrm(ks[15], (BRANCH,), 0.05),
        "l1_w_out": nrm(ks[16], (MIX_WIDTH, D_MODEL), DN_BETA * MIX_WIDTH ** -0.5),
        "l1_ln_g": 1.0 + nrm(ks[17], (D_MODEL,), 0.1),
        "l1_ln_b": nrm(ks[18], (D_MODEL,), 0.02),
    }


def reference(x, positions, l0_w_in, l0_w_a2, l0_b_a, l0_gla_norm_g, l0_sgu_ln_g, l0_sgu_ln_b,
              l0_w_s, l0_b_s, l0_w_out, l0_ln_g, l0_ln_b, l1_w_in, l1_ret_norm_g, l1_w_pool,
              l1_pool_scale, l1_w_out, l1_ln_g, l1_ln_b):
    even_params = [(l0_w_in, l0_w_a2, l0_b_a, l0_gla_norm_g, l0_sgu_ln_g, l0_sgu_ln_b,
                    l0_w_s, l0_b_s, l0_w_out, l0_ln_g, l0_ln_b)]
    odd_params = [(l1_w_in, l1_ret_norm_g, l1_w_pool, l1_pool_scale, l1_w_out, l1_ln_g, l1_ln_b)]
    for layer in range(DEPTH):
        if layer % 2 == 0:
            x = _even_layer(x, *even_params[layer // 2])
        else:
            x = _odd_layer(x, positions, *odd_params[layer // 2])
    return x
```

```python
import contextlib
import math
import numpy as np
import ml_dtypes
import concourse.bass as bass
import concourse.mybir as mybir
from concourse.bass_utils import run_bass_kernel_spmd

F32 = mybir.dt.float32
BF16 = mybir.dt.bfloat16
I32 = mybir.dt.int32
ALU = mybir.AluOpType
AF = mybir.ActivationFunctionType
AX = mybir.AxisListType

NCORES = 8
D = 1024
TOK = 4096
NT = TOK // 128
import os as _os
NT_RUN = int(_os.environ.get('K_NT', NT))
STOP = int(_os.environ.get('K_STOP', 99))
DBG = int(_os.environ.get('K_DBG', 0))
ALPHA = float(4 ** 0.25)
EPS = 1e-5
GC0 = math.sqrt(2.0 / math.pi)
GC1 = 0.044715 * GC0


class Op:
    pass


class Prog:
    ENGS = ("pe", "act", "dve", "pool", "sp")

    def __init__(self):
        self.ops = []
        self.cnt = {}
        self.semh = {}

    def add(self, eng, fn, reads=(), writes=(), dma=None, inc=16, xdeps=()):
        op = Op()
        op.eng, op.fn, op.reads, op.writes, op.dma = eng, fn, tuple(reads), tuple(writes), dma
        op.inc = inc
        op.xdeps = list(xdeps)
        op.need = False
        op.deps = []
        self.ops.append(op)
        return op

    def capture(self, fn):
        n = len(self.ops)
        fn()
        out = self.ops[n:]
        del self.ops[n:]
        return out

    @staticmethod
    def merge(a, b):
        if _os.environ.get("K_MERGE") == "pe" and a and b:
            wa = [1.0 if o.eng == "pe" else 0.05 for o in a]
            wb = [1.0 if o.eng == "pe" else 0.05 for o in b]
            ta, tb = sum(wa), sum(wb)
            out, i, j, ca, cb = [], 0, 0, 0.0, 0.0
            while i < len(a) or j < len(b):
                if j >= len(b) or (i < len(a) and ca / ta <= cb / tb):
                    out.append(a[i]); ca += wa[i]; i += 1
                else:
                    out.append(b[j]); cb += wb[j]; j += 1
            return out
        out, i, j = [], 0, 0
        while i < len(a) or j < len(b):
            if j >= len(b) or (i < len(a) and i * len(b) <= j * len(a)):
                out.append(a[i]); i += 1
            else:
                out.append(b[j]); j += 1
        return out

    def barrier(self):
        return
        last = {}
        for op in self.ops:
            if op.fn is None:
                continue
            if op.dma is not None and not (op.dma.startswith("os") or op.dma.startswith("of") or op.dma.startswith("cc")):
                continue
            last[op.dma if op.dma is not None else op.eng] = op
        for e in self.ENGS:
            self.add(e, None, xdeps=list(last.values()))

    def plan(self):
        last_w, readers = {}, {}
        for op in self.ops:
            deps = []
            for k in op.reads:
                w = last_w.get(k)
                if w is not None:
                    deps.append((w, "RAW"))
            for k in op.writes:
                w = last_w.get(k)
                if w is not None:
                    deps.append((w, "WAW"))
                for r in readers.get(k, ()):
                    deps.append((r, "WAR"))
            for p in op.xdeps:
                if p not in op.deps:
                    op.deps.append(p)
                    p.need = True
            for p, kind in deps:
                if p is op:
                    continue
                if p.dma is None and op.dma is None and p.eng == op.eng:
                    if op.eng == "pe" or kind != "RAW":
                        continue
                if p not in op.deps:
                    op.deps.append(p)
                    p.need = True
            for k in op.reads:
                readers.setdefault(k, []).append(op)
            for k in op.writes:
                last_w[k] = op
                readers[k] = []
        cnt = self.cnt
        for op in self.ops:
            if op.dma is not None:
                cnt[op.dma] = cnt.get(op.dma, 0) + op.inc
                op.sem, op.val = op.dma, cnt[op.dma]
            elif op.need:
                cnt[op.eng] = cnt.get(op.eng, 0) + 1
                op.sem, op.val = op.eng, cnt[op.eng]
        self.sem_names = sorted(set(cnt.keys()) | set(self.ENGS))

    def emit(self, nc, es):
        self.plan()
        for n in self.sem_names:
            if n not in self.semh:
                self.semh[n] = es.enter_context(nc.semaphore("s_" + n))
        semh = self.semh
        by_eng = {e: [op for op in self.ops if op.eng == e] for e in self.ENGS}

        def run(eh, eng):
            waited = {}
            for op in by_eng[eng]:
                wl = {}
                for p in op.deps:
                    wl[p.sem] = max(wl.get(p.sem, 0), p.val)
                for s, v in wl.items():
                    if waited.get(s, 0) >= v:
                        continue
                    eh.wait_ge(semh[s], v)
                    waited[s] = v
                if op.fn is None:
                    continue
                ins = op.fn(eh)
                if op.dma is not None:
                    ins.then_inc(semh[op.dma], op.inc)
                elif op.need:
                    ins.then_inc(semh[eng], 1)

        with nc.Block() as block:
            @block.tensor
            def _(e):
                run(e, "pe")

            @block.scalar
            def _(e):
                run(e, "act")

            @block.vector
            def _(e):
                run(e, "dve")

            @block.gpsimd
            def _(e):
                run(e, "pool")

            @block.sync
            def _(e):
                run(e, "sp")
        self.ops = []


def _pack(items):
    offs, cols, o = {}, [], 0
    for name, arr in items:
        arr = np.ascontiguousarray(arr, dtype=np.float32).reshape(128, -1)
        offs[name] = (o, arr.shape[1])
        cols.append(arr)
        o += arr.shape[1]
    return np.concatenate(cols, axis=1), offs


def _rep(v):
    v = np.asarray(v, np.float32).reshape(1, -1)
    return np.broadcast_to(v, (128, v.shape[1]))


def l0_pack(w_a2, b_a, gla_g, sgu_g, sgu_b, w_s, b_s, ln_g, ln_b):
    j = np.arange(128)[:, None]
    i = np.arange(128)[None, :]
    same = (j // 64) == (i // 64)
    mincl = np.where(same & (j <= i), -1.0 / 16.0, 0.0)
    mafter = np.where(same & (j > i), -1.0 / 16.0, 0.0)
    cm = np.where(same & (j <= i), 1.0, 0.0)
    wa2 = np.zeros((128, 256), np.float32)
    wa2[0:16] = w_a2
    wa2[16] = b_a
    items = [
        ("ident", np.eye(128)),
        ("mincl", mincl), ("mafter", mafter),
        ("negs", np.stack([np.where(np.arange(128) < 64, -1.0 / 16.0, 0.0), np.where(np.arange(128) >= 64, -1.0 / 16.0, 0.0)], axis=1)),
        ("cmask", np.tile(cm, (1, 4))),
        ("gainA", _rep(gla_g.reshape(-1))),
        ("sguG", _rep(sgu_g.reshape(-1))), ("sguB", _rep(sgu_b.reshape(-1))),
        ("wsT", np.transpose(w_s, (2, 0, 1)).reshape(128, 512)),
        ("trilT", np.tile(np.where(j <= i, 1.0, 0.0), (1, 4))),
        ("bs", np.transpose(b_s, (1, 0))),
        ("lng", _rep(ln_g)), ("lnb", _rep(ln_b)),
        ("wa2", wa2),
        ("neghalf", np.full((128, 4), -0.5)),
        ("ones", np.ones((128, 128))),
    ]
    return _pack(items)


def l1_pack(ret_g, w_pool, pool_scale, ln_g, ln_b, first_half):
    H, C = 4, 128
    lg = np.log(1.0 - 2.0 ** (-5.0 - np.arange(H, dtype=np.float64)))
    idx = np.arange(C, dtype=np.float64)
    half = 32
    inv = 10000.0 ** (-np.arange(half, dtype=np.float64) / half) / (2.0 * np.pi)
    invR = np.tile(inv[None, :], (8, 1)).reshape(-1)
    jj = idx[:, None]
    ii = idx[None, :]
    dm = np.zeros((128, H, 128))
    for h in range(H):
        dm[:, h, :] = np.where(ii >= jj, np.exp(np.maximum(ii - jj, 0.0) * lg[h]), 0.0) / 8.0
    winter = np.exp((idx[:, None] + 1.0) * lg[None, :]) / 8.0
    wstate = np.exp((C - 1.0 - idx)[:, None] * lg[None, :])
    winterR = np.repeat(winter[:, :, None], 64, axis=2).reshape(128, 256)
    wstateR = np.repeat(wstate[:, :, None], 64, axis=2).reshape(128, 256)
    gd = np.exp(C * lg)
    gdec = np.broadcast_to(gd[None, :], (128, 4))
    wins = (2, 4, 8, 16)
    s = np.arange(128)[:, None]
    t = np.arange(128)[None, :]
    gcur = np.zeros((128, 4, 128)); gprev = np.zeros((128, 4, 128)); icn = np.zeros((128, 4, 128))
    gcur0 = np.zeros((128, 4, 128)); gprev0 = np.zeros((128, 4, 128)); icn0 = np.zeros((128, 4, 128))
    for g, w in enumerate(wins):
        ind = ((s <= t) & (s > t - w)).astype(np.float64)
        gcur[:, g, :] = ind - w * (s == t)
        gprev[:, g, :] = ((s - 128) > (t - w)).astype(np.float64)
        icn[:, g, :] = 1.0 / w
        if first_half:
            cntt = np.minimum(t + 1, w).astype(np.float64)
            gcur0[:, g, :] = ind - cntt * (s == t)
            gprev0[:, g, :] = 0.0
            icn0[:, g, :] = np.broadcast_to(1.0 / cntt, (128, 128))
        else:
            gcur0[:, g, :] = gcur[:, g, :]; gprev0[:, g, :] = gprev[:, g, :]; icn0[:, g, :] = icn[:, g, :]
    items = [
        ("ident", np.eye(128)),
        ("invR", _rep(invR)),
        ("dmatT", dm.reshape(128, 512)),
        ("winterR", winterR), ("wstateR", wstateR),
        ("gdec", gdec),
        ("gainC", _rep(ret_g.reshape(-1))),
        ("gcur", gcur.reshape(128, 512)), ("gprev", gprev.reshape(128, 512)), ("icn", icn.reshape(128, 512)),
        ("gcur0", gcur0.reshape(128, 512)), ("gprev0", gprev0.reshape(128, 512)), ("icn0", icn0.reshape(128, 512)),
        ("wpool", np.transpose(w_pool, (1, 0, 2)).reshape(128, 512)),
        ("pscale", _rep(pool_scale)),
        ("lng", _rep(ln_g)), ("lnb", _rep(ln_b)),
        ("neghalf", np.full((128, 4), -0.5)),
    ]
    return _pack(items)


class B:
    def __init__(self):
        self.nc = bass.Bass("TRN2", target_bir_lowering=False)
        self.P = Prog()
        self.es = contextlib.ExitStack()
        self.pes = None
        self.pfx = ""
        self.offs = None
        self.cp = None

    def sb(self, name, shape, dt=F32):
        return self.pes.enter_context(self.nc.sbuf_tensor("sb_" + self.pfx + name, list(shape), dt))

    def tsb(self, name, shape, dt=F32, es=None):
        return (es or self.es).enter_context(self.nc.sbuf_tensor("sb_" + name, list(shape), dt))

    def ps(self, name, shape, dt=F32):
        return self.es.enter_context(self.nc.psum_tensor("ps_" + name, list(shape), dt))

    def dram(self, name, shape, dt, kind="Internal"):
        return self.nc.dram_tensor(name, list(shape), dt, kind=kind)

    def c(self, name, lo=0, hi=None, rows=slice(None)):
        o, n = self.offs[name]
        hi = n if hi is None else hi
        return self.cp[rows, o + lo:o + hi]


def phase(b, mode, env):
    nc, P = b.nc, b.P
    L = 0 if mode.startswith("L0") else 1
    full = mode.endswith("F")
    b.pfx = mode + "_"
    STOPL = int(_os.environ.get(f"K_STOP{L}", STOP))
    cp_, offs_ = b.cp, b.offs

    def c_(name, lo=0, hi=None, rows=slice(None)):
        o, n = offs_[name]
        hi = n if hi is None else hi
        return cp_[rows, o + lo:o + hi]
    xin, yout = env["xin"], env.get("yout")
    wmf, identb = env["wm"], env["identb"]
    wal, wo = env.get("wal"), env.get("wo")
    posf = env.get("posf")
    flg = env["flg"]
    tp, b1, b2, b3, b4, b5, b6, b7 = env["psum"]
    xkey = env["xkey"]
    ykey = env.get("ykey")
    with contextlib.ExitStack() as pes:
        b.pes = pes
        x32 = [b.sb(f"x32_{i}", [128, D]) for i in range(2)]
        xb = [b.sb(f"xb_{i}", [128, D], BF16) for i in range(2)]
        xT = [b.sb(f"xT_{i}", [128, 8, 128], BF16) for i in range(2)]
        vbs = [b.sb(f"vb_{i}", [128, 512], BF16) for i in range(2)]
        S32 = b.sb("S32", [64, 4, 128])
        Sb = [b.sb(f"Sb_{i}", [64, 4, 128], BF16) for i in range(2)]
        kst = b.sb("kst", [128, 256], BF16)
        kstc = [b.sb(f"kstc_{i}", [128, 256], BF16) for i in range(2)]
        if L == 0:
            alT = b.sb("alT", [32, 128])
            e1 = b.sb("e1", [128, 256])
            spl = b.sb("spl", [128, 256])
            Eq = b.sb("Eq", [128, 256]); Ek = b.sb("Ek", [128, 256]); Es = b.sb("Es", [128, 256])
            dec = b.sb("dec", [64, 8])
        else:
            ang = b.sb("ang", [128, 512])
            angi = b.sb("angi", [128, 512], I32)
            angf = b.sb("angf", [128, 512])
            gt1 = b.sb("gt1", [128, 512])
            sc_ = b.sb("sincos", [128, 512])
            rt = [b.sb(f"rt{i}", [128, 256]) for i in range(4)]
            qkr = b.sb("qkr", [128, 512])
            plst = b.sb("plst", [128, 512])
        if full:
            qkd = b.sb("qkd", [128, 768], BF16)
            qkT = b.sb("qkT", [64, 12, 128], BF16)
            qm = [b.sb(f"qm_{i}", [64, 4, 128], BF16) for i in range(2)]
            scT = b.sb("scT", [128, 512], BF16)
            sq = b.sb("sq", [128, 512])
            ss = b.sb("ss", [128, 4]); vv = b.sb("vv", [128, 4]); rstd = b.sb("rstd", [128, 4])
            oa1 = b.sb("oa1", [128, 512]); oa2 = b.sb("oa2", [128, 512])
            sg1 = b.sb("sg1", [128, 512]); sg2 = b.sb("sg2", [128, 512])
            cat = b.sb("cat", [128, D], BF16)
            catT = b.sb("catT", [128, 8, 128], BF16)
            bst = b.sb("bst", [128, 4, 6]); bmv = b.sb("bmv", [128, 4, 2])
            bstS = b.sb("bstS", [128, 4, 6]); bmvS = b.sb("bmvS", [128, 4, 2])
            vvS = b.sb("vvS", [128, 4]); rstdS = b.sb("rstdS", [128, 4])
            r32 = b.sb("r32", [128, D])
            bst2 = b.sb("bst2", [128, 12]); bmv2 = b.sb("bmv2", [128, 2])
            vv2 = b.sb("vv2", [128, 1]); rstd2 = b.sb("rstd2", [128, 1])
            n32 = b.sb("n32", [128, D])
            o32 = [b.sb(f"o32_{i}", [128, D]) for i in range(2)]
            if L == 0:
                wsTb = b.sb("wsTb", [128, 4, 128], BF16)
                wsTm = b.sb("wsTm", [128, 512])
                squ = b.sb("squ", [128, 512]); inn = b.sb("inn", [128, 512])
                gu = b.sb("gu", [128, 512]); gsv = b.sb("gsv", [128, 512])
                nrm = b.sb("nrm", [128, 512]); nrm2 = b.sb("nrm2", [128, 512])
                svn = b.sb("svn", [128, 512], BF16)
                t1 = b.sb("t1", [128, 512])
            else:
                pbuf = [b.sb(f"pb_{i}", [128, 512], BF16) for i in range(2)]
                pinit = b.sb("pinit", [128, 512])
                gcb = [b.sb(f"gcb_{i}", [128, 4, 128], BF16) for i in range(2)]
                gpb = [b.sb(f"gpb_{i}", [128, 4, 128], BF16) for i in range(2)]
                wpb = b.sb("wpb", [128, 4, 128], BF16)
                plT = b.sb("plT", [128, 4, 128], BF16)
                on = b.sb("on", [128, 512])

        KX = _os.environ.get("K_X", "")
        if full and not ("1" in KX and L == 0):
            sini = b.sb("sini", [64, 4, 128])
            P.add("sp", lambda e: e.dma_start(out=sini[:, :, :], in_=env["sinit"].rearrange("p (a b) -> p a b", a=4)),
                  reads=[env["sinit_key"]], writes=["sini"], dma="c1" + mode)
            P.add("dve", lambda e: e.tensor_scalar(S32[:, :, :], sini[:, :, :], flg[0:64, 0:1], None, ALU.mult),
                  reads=["sini", "flg"], writes=["S32"])
        else:
            P.add("dve", lambda e: e.memset(S32[:, :, :], 0.0), writes=["S32"])
        P.add("act", lambda e: e.copy(Sb[0][:, :, :], S32[:, :, :]), reads=["S32"], writes=["Sb0"])
        if L == 0:
            P.add("dve", lambda e: e.memset(alT[:, :], 1.0), writes=["alT"])
            for i in range(2):
                if "2" in KX:
                    continue
                P.add("dve", lambda e, i=i: e.memset(kstc[i][:, :], 0.0), writes=[f"kstc{i}"])
                if full:
                    P.add("dve", lambda e, i=i: e.memset(qm[i][:, :, :], 0.0), writes=[f"qm{i}"])
            if full and "3" not in KX:
                P.add("dve", lambda e: e.tensor_tensor(wsTm[:, :], c_("wsT"), c_("trilT"), ALU.mult),
                      reads=["cp"], writes=["wsTm"])
                P.add("dve", lambda e: e.tensor_copy(wsTb[:, :, :], wsTm[:, :].rearrange("p (g t) -> p g t", g=4)),
                      reads=["wsTm"], writes=["wsTb"])
        else:
            if full:
                P.add("sp", lambda e: e.dma_start(out=pinit[:, :], in_=env["pprev"]), reads=[env["sinit_key"]], writes=["pinit"], dma="c3")
                P.add("dve", lambda e: e.tensor_scalar(pbuf[1][:, :], pinit[:, :], flg[:, 0:1], None, ALU.mult),
                      reads=["pinit", "flg"], writes=["pb1"])
                for i, (gc_, gp_) in enumerate((("gcur0", "gprev0"), ("gcur", "gprev"))):
                    P.add("dve", lambda e, i=i, gc_=gc_: e.tensor_copy(
                        gcb[i][:, :, :], c_(gc_).rearrange("p (g t) -> p g t", g=4)), reads=["cp"], writes=[f"gcb{i}"])
                    P.add("dve", lambda e, i=i, gp_=gp_: e.tensor_copy(
                        gpb[i][:, :, :], c_(gp_).rearrange("p (g t) -> p g t", g=4)), reads=["cp"], writes=[f"gpb{i}"])
                P.add("dve", lambda e: e.tensor_copy(wpb[:, :, :], c_("wpool").rearrange("p (g t) -> p g t", g=4)),
                      reads=["cp"], writes=["wpb"])

        def proj(sl, off, n, out_ap, key, extra_reads=()):
            for kc in range(8):
                wap, wkey = wmf(kc, off, n)
                P.add("pe", lambda e, kc=kc, wap=wap: e.matmul(out_ap, xT[sl][:, kc, :], wap,
                                                               start=(kc == 0), stop=(kc == 7)),
                      reads=[f"xT{'a' if kc < 4 else 'b'}{sl}", wkey] + list(extra_reads), writes=[key])

        def load_x(t):
            sl = t % 2
            P.add("sp", lambda e: e.dma_start(out=x32[sl][:, :], in_=xin[t * 128:(t + 1) * 128, :]),
                  reads=([f"{xkey}{t}"] if xkey else []), writes=[f"x32_{sl}"], dma=f"xs{sl}")

        def tile(t, part="all"):
            sl = t % 2
            doH, doG, doS, doT = (part in ("all", x) for x in "HGST")
            if part == "all":
                tpx, tpk = tp, "tp"
            elif full:
                tpx, tpk = b3[:, :].bitcast(BF16).rearrange("p (a c) -> p a c", a=8), "b3"
            else:
                tpx, tpk = b5[:, :].bitcast(BF16).rearrange("p (a c) -> p a c", a=8), "b5"
            if STOPL <= 0:
                return
            early_g1 = full and part != "all" and not _os.environ.get("K_LATEG1")
            if doH:
              head_part(t, sl, tpx, tpk)
              if early_g1:
                g1_part(t, sl)
            if doG:
              if not (early_g1 and part == "G"):
                g1_part(t, sl)
              g2_part(t, sl)
            if full and doS:
              s_part(t, sl)
            if full and doT:
              tail_part(t, sl)

        def pbanks(t):
            if full or t % 2 == 0 or _os.environ.get("K_NOTHREAD"):
                return b1, b2, "b1", "b2"
            return b6, b7, "b6", "b7"

        def head_part(t, sl, tpx, tpk):
            for _ in range(3):
                if b.bg:
                    b.bg.pop(0)()
            if tpk == "tp" and t + 1 < NT_RUN:
                load_x(t + 1)
            P.add("dve", lambda e, sl=sl: e.tensor_copy(xb[sl][:, :], x32[sl][:, :]),
                  reads=[f"x32_{sl}"], writes=[f"xb{sl}"])
            for kc in range(8):
                P.add("pe", lambda e, kc=kc, sl=sl: e.transpose(tpx[:, kc, :], xb[sl][:, kc * 128:(kc + 1) * 128], identb[:, :]),
                      reads=[f"xb{sl}", "identb"], writes=[tpk])
            P.add("act", lambda e, sl=sl: e.copy(xT[sl][:, 0:4, :], tpx[:, 0:4, :]), reads=[tpk], writes=[f"xTa{sl}"])
            P.add("act", lambda e, sl=sl: e.copy(xT[sl][:, 4:8, :], tpx[:, 4:8, :]), reads=[tpk], writes=[f"xTb{sl}"])

            if STOPL <= 1:
                return
            bq, bv, bqk, bvk = pbanks(t)
            if full:
                proj(sl, 0, 512, bq[:, :], bqk)
            else:
                proj(sl, 256, 256, bq[:, 256:512], bqk)
            proj(sl, 512, 512, bv[:, :], bvk)
            P.add("act", lambda e: e.copy(vbs[sl][:, :], bv[:, :]), reads=[bvk], writes=[f"vb{sl}"])
            if full:
                proj(sl, 1024, 512, b6[:, :], "b6")
                proj(sl, 1536, 512, b7[:, :], "b7")
                if L == 0:
                    P.add("act", lambda e: e.activation(sg1[:, :], b6[:, :], AF.Sigmoid), reads=["b6"], writes=["sg1"])
                    P.add("dve", lambda e: e.tensor_tensor(sg2[:, :], b6[:, :], sg1[:, :], ALU.mult), reads=["b6", "sg1"], writes=["sg2"])
                    P.add("pool", lambda e: e.tensor_tensor(oa2[:, :], sg2[:, :], c_("gainA"), ALU.mult), reads=["sg2", "cp"], writes=["gs"])
                else:
                    P.add("act", lambda e: e.activation(sg2[:, :], b6[:, :], AF.Silu), reads=["b6"], writes=["sg2"])
                    P.add("pool", lambda e: e.tensor_tensor(oa2[:, :], sg2[:, :], c_("gainC"), ALU.mult), reads=["sg2", "cp"], writes=["gs"])

        def g1_part(t, sl):
            bq, bv, bqk, bvk = pbanks(t)
            if STOPL <= 2:
                return
            if L == 0:
                for kc in range(8):
                    P.add("pe", lambda e, kc=kc, sl=sl: e.matmul(b3[0:16, 256:384], wal[:, kc, :], xT[sl][:, kc, :],
                                                                 start=(kc == 0), stop=(kc == 7)),
                          reads=[f"xT{'a' if kc < 4 else 'b'}{sl}", "wal"], writes=["b3"])
                P.add("dve", lambda e: e.tensor_copy(alT[0:16, :], b3[0:16, 256:384]), reads=["b3"], writes=["alT"])
                P.add("pe", lambda e: e.matmul(b3[:, 0:256], alT[0:17, :], c_("wa2", rows=slice(0, 17)), start=True, stop=True),
                      reads=["alT", "cp"], writes=["b3"])
                P.add("act", lambda e: e.activation(e1[:, :], b3[:, 0:256], AF.Exp, scale=-1.0), reads=["b3"], writes=["e1"])
                P.add("act", lambda e: e.activation(spl[:, :], e1[:, :], AF.Ln, bias=1.0), reads=["e1"], writes=["spl"])
                P.add("pe", lambda e: e.matmul(b4[:, 0:256], c_("mincl"), spl[:, :], start=True, stop=True),
                      reads=["spl", "cp"], writes=["b4"])
                P.add("pe", lambda e: e.matmul(b4[:, 256:512], c_("mafter"), spl[:, :], start=True, stop=True),
                      reads=["spl", "cp"], writes=["b4"])
                for h in range(4):
                    P.add("pe", lambda e, h=h: e.matmul(
                        b3[0:64, 384 + h * 2:386 + h * 2], spl[:, h * 64:(h + 1) * 64],
                        c_("negs", 0, 2), start=True, stop=True),
                        reads=["spl", "cp"], writes=["b3"])
                if full:
                    P.add("act", lambda e: e.activation(Eq[:, :], b4[:, 0:256], AF.Exp), reads=["b4"], writes=["Eq"])
                    P.add("act", lambda e: e.activation(Ek[:, :], b4[:, 0:256], AF.Exp, scale=-1.0), reads=["b4"], writes=["Ek"])
                P.add("act", lambda e: e.activation(Es[:, :], b4[:, 256:512], AF.Exp), reads=["b4"], writes=["Es"])
                P.add("act", lambda e: e.activation(dec[:, :], b3[0:64, 384:392], AF.Exp), reads=["b3"], writes=["dec"])
                if full:
                    P.add("dve", lambda e: e.scalar_tensor_tensor(qkd[:, 0:256], b1[:, 0:256], 0.125, Eq[:, :], ALU.mult, ALU.mult),
                          reads=["b1", "Eq"], writes=["qkd"])
                    P.add("dve", lambda e: e.tensor_tensor(qkd[:, 256:512], b1[:, 256:512], Ek[:, :], ALU.mult),
                          reads=["b1", "Ek"], writes=["qkd"])
                for c in range(2):
                    P.add("dve", lambda e, c=c: e.tensor_tensor(kstc[c][64 * c:64 * c + 64, :], bq[64 * c:64 * c + 64, 256:512],
                                                                Es[64 * c:64 * c + 64, :], ALU.mult),
                          reads=[bqk, "Es"], writes=[f"kstc{c}"])
                nblk = 8
            else:
                if t % 8 == 0:
                    n8 = min(8, NT_RUN - t)
                    w8 = n8 * 64
                    angv = ang[:, 0:w8].rearrange("p (j s f) -> p j s f", s=2, f=32)
                    P.add("dve", lambda e: e.tensor_tensor(
                        angv[:, :, 0, :], posf[:, t:t + n8].unsqueeze(2).broadcast_to([128, n8, 32]),
                        c_("invR", 0, 32).unsqueeze(1).broadcast_to([128, n8, 32]), ALU.mult),
                        reads=["cp", "posf"], writes=["ang"])
                    P.add("dve", lambda e: e.tensor_scalar(angv[:, :, 1, :], angv[:, :, 0, :], 0.25, None, ALU.add),
                          reads=["ang"], writes=["ang"])
                    P.add("dve", lambda e: e.tensor_copy(angi[:, 0:w8], ang[:, 0:w8]), reads=["ang"], writes=["angi"])
                    P.add("dve", lambda e: e.tensor_copy(angf[:, 0:w8], angi[:, 0:w8]), reads=["angi"], writes=["angf"])
                    P.add("dve", lambda e: e.tensor_tensor(ang[:, 0:w8], ang[:, 0:w8], angf[:, 0:w8], ALU.subtract),
                          reads=["ang", "angf"], writes=["ang"])
                    P.add("dve", lambda e: e.tensor_scalar(gt1[:, 0:w8], ang[:, 0:w8], 0.5, None, ALU.is_gt), reads=["ang"], writes=["gt1"])
                    P.add("dve", lambda e: e.tensor_tensor(ang[:, 0:w8], ang[:, 0:w8], gt1[:, 0:w8], ALU.subtract),
                          reads=["ang", "gt1"], writes=["ang"])
                    P.add("dve", lambda e: e.tensor_scalar(gt1[:, 0:w8], ang[:, 0:w8], -0.5, None, ALU.is_lt), reads=["ang"], writes=["gt1"])
                    P.add("dve", lambda e: e.tensor_tensor(ang[:, 0:w8], ang[:, 0:w8], gt1[:, 0:w8], ALU.add),
                          reads=["ang", "gt1"], writes=["ang"])
                    P.add("act", lambda e: e.activation(sc_[:, 0:w8], ang[:, 0:w8], AF.Sin, scale=2.0 * math.pi), reads=["ang"], writes=["sincos"])
                lo = 0 if full else 4
                nh = 8 - lo

                def v4(ap):
                    return ap.rearrange("p (a two c) -> p a two c", two=2, c=32)
                hq = v4(bq[:, :])
                j8 = (t % 8) * 64
                sinv = sc_[:, j8:j8 + 32].unsqueeze(1).broadcast_to([128, 8, 32])
                cosv = sc_[:, j8 + 32:j8 + 64].unsqueeze(1).broadcast_to([128, 8, 32])
                rtv = [r[:, :].rearrange("p (a c) -> p a c", c=32) for r in rt]
                qv = v4(qkr[:, :])
                for i_, (half_, tab) in enumerate(((0, cosv), (1, sinv), (0, sinv), (1, cosv))):
                    P.add("dve", lambda e, i_=i_, half_=half_, tab=tab: e.tensor_tensor(
                        rtv[i_][:, lo:8, :], hq[:, lo:8, half_, :], tab[:, lo:8, :], ALU.mult),
                        reads=[bqk, "sincos"], writes=[f"rt{i_}"])
                P.add("dve", lambda e: e.tensor_tensor(qv[:, lo:8, 0, :], rtv[0][:, lo:8, :], rtv[1][:, lo:8, :], ALU.subtract),
                      reads=["rt0", "rt1"], writes=["qkr"])
                P.add("dve", lambda e: e.tensor_tensor(qv[:, lo:8, 1, :], rtv[2][:, lo:8, :], rtv[3][:, lo:8, :], ALU.add),
                      reads=["rt2", "rt3"], writes=["qkr"])
                P.add("pool", lambda e: e.tensor_tensor(kst[:, :], qkr[:, 256:512], c_("wstateR"), ALU.mult),
                      reads=["qkr", "cp"], writes=["kst"])
                if full:
                    P.add("act", lambda e: e.copy(qkd[:, 0:512], qkr[:, :]), reads=["qkr"], writes=["qkd"])
                    P.add("dve", lambda e: e.tensor_tensor(qkd[:, 512:768], qkr[:, 0:256], c_("winterR"), ALU.mult),
                          reads=["qkr", "cp"], writes=["qkd"])
                nblk = 12

        def g2_part(t, sl):
            bq, bv, bqk, bvk = pbanks(t)
            nblk = 8 if L == 0 else 12
            if STOPL <= 3:
                return
            def l0_U(c):
                for h in range(4):
                    P.add("pe", lambda e, h=h, c=c: e.matmul(
                        b4[0:64, h * 128:(h + 1) * 128], kstc[c][:, h * 64:(h + 1) * 64], vbs[sl][:, h * 128:(h + 1) * 128],
                        start=True, stop=True), reads=[f"kstc{c}", f"vb{sl}"], writes=["b4"])

            def l0_update(c):
                for h in range(4):
                    P.add("dve", lambda e, c=c, h=h: e.scalar_tensor_tensor(
                        S32[:, h, :], S32[:, h, :], dec[:, h * 2 + c:h * 2 + c + 1],
                        b4[0:64, h * 128:(h + 1) * 128], ALU.mult, ALU.add),
                        reads=["S32", "dec", "b4"], writes=["S32"])
                nxt = 1 - c
                if full:
                    P.add("act", lambda e, nxt=nxt: e.copy(Sb[nxt][:, :, :], S32[:, :, :]), reads=["S32"], writes=[f"Sb{nxt}"])

            if L == 0:
                l0_U(0)
                if STOPL <= 4:
                    return
                l0_update(0)
                if STOPL <= 5:
                    return
            if full:
                for blk in range(8):
                    P.add("pe", lambda e, blk=blk: e.transpose(tp[0:64, blk, :], qkd[:, blk * 64:(blk + 1) * 64], identb[:, :]),
                          reads=["qkd", "identb"], writes=["tp"])
                P.add("act", lambda e: e.copy(qkT[:, 0:8, :], tp[0:64, 0:8, :]), reads=["tp"], writes=["qkT"])
                if L == 0:
                    for c in range(2):
                        P.add("act", lambda e, c=c: e.copy(qm[c][:, :, 64 * c:64 * c + 64], tp[0:64, 0:4, 64 * c:64 * c + 64]),
                              reads=["tp"], writes=[f"qm{c}"])
                for h in range(4):
                    P.add("pe", lambda e, h=h: e.matmul(
                        b5[:, h * 128:(h + 1) * 128], qkT[:, 4 + h, :], qkT[:, h, :],
                        start=True, stop=True), reads=["qkT"], writes=["b5"])
                if L == 1:
                    for blk in range(4):
                        P.add("pe", lambda e, blk=blk: e.transpose(tp[0:64, blk, :], qkd[:, 512 + blk * 64:512 + (blk + 1) * 64], identb[:, :]),
                              reads=["qkd", "identb"], writes=["tp"])
                    P.add("act", lambda e: e.copy(qkT[:, 8:12, :], tp[0:64, 0:4, :]), reads=["tp"], writes=["qkT2"])
                mk = "cmask" if L == 0 else "dmatT"
                for hp in range(2):
                    P.add("dve", lambda e, hp=hp: e.tensor_tensor(scT[:, hp * 256:(hp + 1) * 256], b5[:, hp * 256:(hp + 1) * 256],
                                                                  c_(mk, hp * 256, (hp + 1) * 256), ALU.mult),
                          reads=["b5", "cp"], writes=[f"scT{hp}"])
                for h in range(4):
                    P.add("pe", lambda e, h=h: e.matmul(b1[:, h * 128:(h + 1) * 128], scT[:, h * 128:(h + 1) * 128],
                                                        vbs[sl][:, h * 128:(h + 1) * 128], start=True, stop=False),
                          reads=[f"scT{h // 2}", f"vb{sl}"], writes=["b1"])
                    if L == 0:
                        for c in range(2):
                            P.add("pe", lambda e, h=h, c=c: e.matmul(
                                b1[:, h * 128:(h + 1) * 128], qm[c][:, h, :], Sb[c][:, h, :], start=False, stop=(c == 1)),
                                reads=[f"qm{c}", f"Sb{c}"], writes=["b1"])
                    else:
                        P.add("pe", lambda e, h=h: e.matmul(
                            b1[:, h * 128:(h + 1) * 128], qkT[:, 8 + h, :], Sb[0][:, h, :],
                            start=False, stop=True), reads=["qkT2", "Sb0"], writes=["b1"])

            if L == 0:
                l0_U(1)
                l0_update(1)
            else:
                for h in range(4):
                    P.add("pe", lambda e, h=h: e.matmul(
                        b4[0:64, h * 128:(h + 1) * 128], kst[:, h * 64:(h + 1) * 64],
                        vbs[sl][:, h * 128:(h + 1) * 128], start=True, stop=True), reads=["kst", f"vb{sl}"], writes=["b4"])
                for h in range(4):
                    P.add("dve", lambda e, h=h: e.scalar_tensor_tensor(
                        S32[:, h, :], S32[:, h, :], c_("gdec", h, h + 1, rows=slice(0, 64)), b4[0:64, h * 128:(h + 1) * 128], ALU.mult, ALU.add),
                        reads=["S32", "cp", "b4"], writes=["S32"])
                if full:
                    P.add("act", lambda e: e.copy(Sb[0][:, :, :], S32[:, :, :]), reads=["S32"], writes=["Sb0"])
                if not full and t == NT_RUN - 1:
                    proj(sl, 1536, 512, b6[:, :], "b6")
                    P.add("act", lambda e: e.copy(plst[:, :], b6[:, :]), reads=["b6"], writes=["plst"])

            if not full:
                return

            if L == 0:
                P.add("act", lambda e: e.activation(sq[:, :], b1[:, :], AF.Square), reads=["b1"], writes=["sq"])
                P.add("dve", lambda e: e.tensor_reduce(ss[:, :], sq[:, :].rearrange("p (h c) -> p h c", h=4), AX.X, ALU.add),
                      reads=["sq"], writes=["ss"])
                P.add("pool", lambda e: e.tensor_scalar(vv[:, :], ss[:, :], 1.0 / 128.0, EPS, ALU.mult, ALU.add), reads=["ss"], writes=["vv"])
                P.add("pool", lambda e: e.tensor_tensor(rstd[:, :], vv[:, :], c_("neghalf"), ALU.pow), reads=["vv", "cp"], writes=["rstd"])
                for h in range(4):
                    P.add("dve", lambda e, h=h: e.scalar_tensor_tensor(
                        cat[:, h * 128:(h + 1) * 128], b1[:, h * 128:(h + 1) * 128], rstd[:, h:h + 1],
                        oa2[:, h * 128:(h + 1) * 128], ALU.mult, ALU.mult), reads=["b1", "rstd", "gs"], writes=["catA"])
            else:
                for h in range(4):
                    P.add("dve", lambda e, h=h: e.bn_stats(bst[:, h, :], b1[:, h * 128:(h + 1) * 128]), reads=["b1"], writes=["bst"])
                for h in range(4):
                    P.add("dve", lambda e, h=h: e.bn_aggr(bmv[:, h, :], bst[:, h, :]), reads=["bst"], writes=["bmv"])
                P.add("pool", lambda e: e.tensor_scalar(vv[:, :], bmv[:, :, 1], EPS, None, ALU.add), reads=["bmv"], writes=["vv"])
                P.add("pool", lambda e: e.tensor_tensor(rstd[:, :], vv[:, :], c_("neghalf"), ALU.pow), reads=["vv", "cp"], writes=["rstd"])
                for h in range(4):
                    P.add("dve", lambda e, h=h: e.tensor_scalar(on[:, h * 128:(h + 1) * 128], b1[:, h * 128:(h + 1) * 128],
                                                                bmv[:, h, 0:1], rstd[:, h:h + 1], ALU.subtract, ALU.mult),
                          reads=["b1", "bmv", "rstd"], writes=["on"])
                P.add("dve", lambda e: e.tensor_tensor(cat[:, 0:512], on[:, :], oa2[:, :], ALU.mult), reads=["on", "gs"], writes=["catA"])

        def s_part(t, sl):
            if L == 0:
                P.add("act", lambda e: e.activation(squ[:, :], b7[:, :], AF.Square, scale=math.sqrt(GC1)), reads=["b7"], writes=["squ"])
                P.add("dve", lambda e: e.scalar_tensor_tensor(inn[:, :], squ[:, :], GC0, b7[:, :], ALU.add, ALU.mult),
                      reads=["squ", "b7"], writes=["inn"])
                P.add("act", lambda e: e.activation(sg1[:, :], inn[:, :], AF.Sigmoid, scale=2.0), reads=["inn"], writes=["sg1"])
                P.add("dve", lambda e: e.tensor_tensor(gu[:, :], b7[:, :], sg1[:, :], ALU.mult), reads=["b7", "sg1"], writes=["gu"])
                proj(sl, 2048, 512, b6[:, :], "b6")
                P.add("act", lambda e: e.activation(squ[:, :], b6[:, :], AF.Square, scale=math.sqrt(GC1)), reads=["b6"], writes=["squ"])
                P.add("dve", lambda e: e.scalar_tensor_tensor(inn[:, :], squ[:, :], GC0, b6[:, :], ALU.add, ALU.mult),
                      reads=["squ", "b6"], writes=["inn"])
                P.add("act", lambda e: e.activation(sg1[:, :], inn[:, :], AF.Sigmoid, scale=2.0), reads=["inn"], writes=["sg1"])
                P.add("dve", lambda e: e.tensor_tensor(gsv[:, :], b6[:, :], sg1[:, :], ALU.mult), reads=["b6", "sg1"], writes=["gsv"])
                for g in range(4):
                    P.add("dve", lambda e, g=g: e.bn_stats(bstS[:, g, :], gsv[:, g * 128:(g + 1) * 128]), reads=["gsv"], writes=["bstS"])
                for g in range(4):
                    P.add("dve", lambda e, g=g: e.bn_aggr(bmvS[:, g, :], bstS[:, g, :]), reads=["bstS"], writes=["bmvS"])
                P.add("pool", lambda e: e.tensor_scalar(vvS[:, :], bmvS[:, :, 1], EPS, None, ALU.add), reads=["bmvS"], writes=["vvS"])
                P.add("pool", lambda e: e.tensor_tensor(rstdS[:, :], vvS[:, :], c_("neghalf"), ALU.pow), reads=["vvS", "cp"], writes=["rstdS"])
                for g in range(4):
                    P.add("dve", lambda e, g=g: e.tensor_scalar(nrm[:, g * 128:(g + 1) * 128], gsv[:, g * 128:(g + 1) * 128],
                                                                bmvS[:, g, 0:1], rstdS[:, g:g + 1], ALU.subtract, ALU.mult),
                          reads=["gsv", "bmvS", "rstdS"], writes=["nrm"])
                P.add("dve", lambda e: e.tensor_tensor(nrm2[:, :], nrm[:, :], c_("sguG"), ALU.mult), reads=["nrm", "cp"], writes=["nrm2"])
                P.add("dve", lambda e: e.tensor_tensor(svn[:, :], nrm2[:, :], c_("sguB"), ALU.add), reads=["nrm2", "cp"], writes=["svn"])
                for g in range(4):
                    P.add("pe", lambda e, g=g: e.matmul(b2[:, g * 128:(g + 1) * 128], wsTb[:, g, :], svn[:, g * 128:(g + 1) * 128],
                                                        start=True, stop=True), reads=["wsTb", "svn"], writes=["b2"])
                for g in range(4):
                    P.add("dve", lambda e, g=g: e.scalar_tensor_tensor(
                        t1[:, g * 128:(g + 1) * 128], b2[:, g * 128:(g + 1) * 128], c_("bs", g, g + 1),
                        gu[:, g * 128:(g + 1) * 128], ALU.add, ALU.mult), reads=["b2", "cp", "gu"], writes=["t1"])
                proj(sl, 2560, 512, b7[:, :], "b7")
                P.add("act", lambda e: e.activation(sg1[:, :], b7[:, :], AF.Sigmoid), reads=["b7"], writes=["sg1"])
                P.add("dve", lambda e: e.tensor_tensor(sg2[:, :], b7[:, :], sg1[:, :], ALU.mult), reads=["b7", "sg1"], writes=["sg2"])
                P.add("dve", lambda e: e.tensor_tensor(cat[:, 512:1024], t1[:, :], sg2[:, :], ALU.mult), reads=["t1", "sg2"], writes=["catB"])
            else:
                cur = t % 2
                P.add("act", lambda e, cur=cur: e.copy(pbuf[cur][:, :], b7[:, :]), reads=["b7"], writes=[f"pb{cur}"])
                ti = 0 if t == 0 else 1
                for g in range(4):
                    P.add("pe", lambda e, g=g, cur=cur, ti=ti: e.matmul(b3[:, g * 128:(g + 1) * 128], pbuf[cur][:, g * 128:(g + 1) * 128],
                                                                        gcb[ti][:, g, :], start=True, stop=False),
                          reads=[f"pb{cur}", f"gcb{ti}"], writes=["b3"])
                    P.add("pe", lambda e, g=g, cur=cur, ti=ti: e.matmul(b3[:, g * 128:(g + 1) * 128], pbuf[1 - cur][:, g * 128:(g + 1) * 128],
                                                                        gpb[ti][:, g, :], start=False, stop=True),
                          reads=[f"pb{1 - cur}", f"gpb{ti}"], writes=["b3"])
                ik = "icn0" if t == 0 else "icn"
                P.add("dve", lambda e, ik=ik: e.tensor_tensor(plT[:, :, :], b3[:, :].rearrange("p (g t) -> p g t", g=4),
                                                              c_(ik).rearrange("p (g t) -> p g t", g=4), ALU.mult),
                      reads=["b3", "cp"], writes=["plT"])
                for g in range(4):
                    P.add("pe", lambda e, g=g: e.matmul(b2[:, g * 128:(g + 1) * 128], plT[:, g, :], wpb[:, g, :], start=True, stop=True),
                          reads=["plT", "wpb"], writes=["b2"])
                proj(sl, 2048, 512, b6[:, :], "b6")
                P.add("act", lambda e: e.activation(sg1[:, :], b6[:, :], (AF.Sigmoid if _os.environ.get("K_NOSILU") else AF.Silu)), reads=["b6"], writes=["sg1"])
                P.add("dve", lambda e: e.tensor_tensor(oa1[:, :], b2[:, :], c_("pscale"), ALU.mult), reads=["b2", "cp"], writes=["oa1"])
                P.add("dve", lambda e: e.tensor_tensor(cat[:, 512:1024], oa1[:, :], sg1[:, :], ALU.mult), reads=["oa1", "sg1"], writes=["catB"])

        def tail_part(t, sl):
            yk = ("b5", "tp")
            ybank = (b5, tp[:, :, :].rearrange("p a c -> p (a c)").bitcast(F32))
            for kc in range(8):
                P.add("pe", lambda e, kc=kc: e.transpose(tp[:, kc, :], cat[:, kc * 128:(kc + 1) * 128], identb[:, :]),
                      reads=["catA" if kc < 4 else "catB", "identb"], writes=["tp"])
            P.add("act", lambda e: e.copy(catT[:, 0:4, :], tp[:, 0:4, :]), reads=["tp"], writes=["catTa"])
            P.add("act", lambda e: e.copy(catT[:, 4:8, :], tp[:, 4:8, :]), reads=["tp"], writes=["catTb"])
            for nb in range(2):
                for kc in range(8):
                    P.add("pe", lambda e, kc=kc, nb=nb: e.matmul(ybank[nb][:, :], catT[:, kc, :], wo[:, kc, nb * 512:(nb + 1) * 512],
                                                                 start=(kc == 0), stop=(kc == 7)),
                          reads=["catTa" if kc < 4 else "catTb", "wo"], writes=[yk[nb]])
                P.add("dve", lambda e, nb=nb, sl=sl: e.scalar_tensor_tensor(
                    r32[:, nb * 512:(nb + 1) * 512], x32[sl][:, nb * 512:(nb + 1) * 512], ALPHA, ybank[nb][:, :], ALU.mult, ALU.add),
                    reads=[f"x32_{sl}", yk[nb]], writes=[f"r32_{nb}"])
                P.add("dve", lambda e, nb=nb: e.bn_stats(bst2[:, nb * 6:(nb + 1) * 6], r32[:, nb * 512:(nb + 1) * 512]), reads=[f"r32_{nb}"], writes=["bst2"])
            P.add("dve", lambda e: e.bn_aggr(bmv2[:, :], bst2[:, :]), reads=["bst2"], writes=["bmv2"])
            P.add("pool", lambda e: e.tensor_scalar(vv2[:, :], bmv2[:, 1:2], EPS, None, ALU.add), reads=["bmv2"], writes=["vv2"])
            P.add("pool", lambda e: e.tensor_tensor(rstd2[:, :], vv2[:, :], c_("neghalf", 0, 1), ALU.pow), reads=["vv2", "cp"], writes=["rstd2"])
            P.add("dve", lambda e: e.scalar_tensor_tensor(n32[:, :], r32[:, :], bmv2[:, 0:1], c_("lng"), ALU.subtract, ALU.mult),
                  reads=["r32_0", "r32_1", "bmv2", "cp"], writes=["n32"])
            P.add("dve", lambda e, sl=sl: e.scalar_tensor_tensor(o32[sl][:, :], n32[:, :], rstd2[:, 0:1], c_("lnb"), ALU.mult, ALU.add),
                  reads=["n32", "rstd2", "cp"], writes=[f"o32_{sl}"])
            P.add("sp", lambda e, sl=sl, t=t: e.dma_start(out=yout[t * 128:(t + 1) * 128, :], in_=o32[sl][:, :]),
                  reads=[f"o32_{sl}"], writes=[f"yout{sl}"] + ([f"{ykey}{t}"] if ykey else []), dma=f"os{sl}")

        if STOPL > 0:
            load_x(0)
        if _os.environ.get("K_NOTHREAD"):
            for t_ in range(NT_RUN):
                tile(t_)
        elif not full:
            if NT_RUN > 1:
                load_x(1)
            P.ops.extend(P.capture(lambda: tile(0, "H")))
            for t_ in range(NT_RUN):
                g_ = P.capture(lambda: tile(t_, "G"))
                hd_ = P.capture(lambda: tile(t_ + 1, "H")) if t_ + 1 < NT_RUN else []
                ngs_ = int(len(g_) * float(_os.environ.get("K_SGF", 1.0)))
                nhs_ = int(len(hd_) * float(_os.environ.get("K_SHF", 0.5)))
                P.ops.extend(Prog.merge(g_[:ngs_], hd_[:nhs_]))
                P.ops.extend(hd_[nhs_:])
                P.ops.extend(g_[ngs_:])
                if t_ + 2 < NT_RUN:
                    load_x(t_ + 2)
        else:
            if NT_RUN > 1:
                load_x(1)
            P.ops.extend(P.capture(lambda: tile(0, "H")))
            for t_ in range(NT_RUN):
                g_ = P.capture(lambda: tile(t_, "G"))
                s_ = P.capture(lambda: tile(t_, "S"))
                ng_ = int(len(g_) * float(_os.environ.get("K_GF", 1.0)))
                ns_ = int(len(s_) * float(_os.environ.get("K_SF", 0.7)))
                P.ops.extend(Prog.merge(g_[:ng_], s_[:ns_]))
                P.ops.extend(g_[ng_:])
                P.ops.extend(s_[ns_:])
                tl_ = P.capture(lambda: tile(t_, "T"))
                hd_ = P.capture(lambda: tile(t_ + 1, "H")) if t_ + 1 < NT_RUN else []
                nh_ = int(len(hd_) * float(_os.environ.get("K_TF", 0.2)))
                P.ops.extend(Prog.merge(tl_, hd_[:nh_]))
                P.ops.extend(hd_[nh_:])
                if t_ + 2 < NT_RUN:
                    load_x(t_ + 2)

        if full:
            P.add("sp", None, reads=["yout0", "yout1"])
        else:
            st = env["st_local"]
            P.add("sp", lambda e: e.dma_start(out=st[0:64, :].rearrange("p (a b) -> p a b", a=4), in_=S32[:, :, :]),
                  reads=["S32"], writes=["st_local"], dma="of")
            if L == 1:
                P.add("sp", lambda e: e.dma_start(out=st[64:192, :], in_=plst[:, :]), reads=["plst"], writes=["st_local"], dma="of")
            if _os.environ.get("K_NOCC"):
                P.add("sp", lambda e: e.dma_start(out=env["st_gath_t"].ap()[0:(64 if L == 0 else 192), :], in_=env["st_local_t"].ap()),
                      reads=["st_local"], writes=[env["st_gath_key"]], dma="of2")
            else:
              P.add("pool", lambda e: e.collective_compute(
                "AllGather", ALU.bypass, replica_groups=[[0, 1], [2, 3], [4, 5], [6, 7]],
                ins=[env["st_local_t"].ap().opt()], outs=[env["st_gath_t"].ap().opt()]),
                reads=["st_local"], writes=[env["st_gath_key"]], dma=f"cc{L}", inc=1)
        last = {}
        for op in P.ops:
            if op.fn is not None and op.dma is not None and not (op.dma.startswith("sg") or op.dma.startswith("xs")):
                last[op.dma] = op
        P.add("sp", None, xdeps=list(last.values()))
        P.emit(nc, b.es)
        b.pes = None


def build():
    b = B()
    nc, P = b.nc, b.P
    cp0_, offs0 = l0_pack(*[np.zeros(s_, np.float32) for s_ in ((16, 256), (256,), (4, 128), (4, 128), (4, 128), (4, 128, 128), (4, 128), (1024,), (1024,))])
    cp1_, offs1 = l1_pack(np.zeros((4, 128), np.float32), np.zeros((4, 128, 128), np.float32), np.zeros((512,), np.float32),
                          np.zeros((1024,), np.float32), np.zeros((1024,), np.float32), True)
    ncp0, ncp1 = cp0_.shape[1], cp1_.shape[1]
    with b.es:
        xin = b.dram("x", [TOK, D], F32, "ExternalInput").ap()
        cp0d = b.dram("cp0", [128, ncp0], F32, "ExternalInput").ap()
        cp1d = b.dram("cp1", [128, ncp1], F32, "ExternalInput").ap()
        w0d = b.dram("w0m", [D, 3072], F32, "ExternalInput").ap()
        wald = b.dram("w0a", [D, 16], F32, "ExternalInput").ap()
        wo0d = b.dram("wo0", [D, D], F32, "ExternalInput").ap()
        w1d = b.dram("w1m", [D, 2560], F32, "ExternalInput").ap()
        wo1d = b.dram("wo1", [D, D], F32, "ExternalInput").ap()
        posd = b.dram("pos", [128, NT], I32, "ExternalInput").ap()
        flgd = b.dram("flag", [128, 1], F32, "ExternalInput").ap()
        yout = b.dram("y", [TOK, D], F32, "ExternalOutput").ap()
        x1s = b.dram("x1s", [TOK, D], F32).ap()
        stA_t = b.dram("stA", [64, 512], F32)
        stG_t = b.dram("stG", [128, 512], F32)
        stB_t = b.dram("stB", [192, 512], F32)
        stH_t = b.dram("stH", [384, 512], F32)
        stA, stG, stB, stH = stA_t.ap(), stG_t.ap(), stB_t.ap(), stH_t.ap()

        psum = (b.ps("tp", [128, 8, 128], BF16),) + tuple(b.ps(f"b{i}", [128, 512]) for i in range(1, 8))
        identb = b.tsb("identb", [128, 128], BF16)
        identf = b.tsb("identf", [128, 128])
        flg = b.tsb("flg", [128, 1])
        posi = b.tsb("posi", [128, NT], I32)
        posf = b.tsb("posf", [128, NT])
        wmA1 = b.tsb("wmA1", [128, 8, 1024], BF16)
        P.add("sp", lambda e: e.dma_start(out=flg[:, :], in_=flgd[:, :]), writes=["flg"], dma="c4")
        P.add("sp", lambda e: e.dma_start(out=posi[:, :], in_=posd[:, :]), writes=["posi"], dma="c2")
        P.add("dve", lambda e: e.tensor_copy(posf[:, :], posi[:, :]), reads=["posi"], writes=["posf"])
        P.add("sp", lambda e: e.dma_start(out=identf[:, :], in_=cp0d[:, 0:128]), writes=["identf"], dma="c5")
        P.add("dve", lambda e: e.tensor_copy(identb[:, :], identf[:, :]), reads=["identf"], writes=["identb"])

        NSTG = 4
        stg = [b.tsb(f"stg{i}", [128, 512]) for i in range(NSTG)]
        b.bg = []
        b.stg_i = 0

        def wload(dst, dcol, src, scol, ncols, key, now=False):
            for kc in range(8):
                for c0 in range(0, ncols, 512):
                    n = min(512, ncols - c0)

                    def job(kc=kc, c0=c0, n=n):
                        i = b.stg_i % NSTG
                        b.stg_i += 1
                        P.add("sp", lambda e: e.dma_start(out=stg[i][:, 0:n], in_=src[kc * 128:(kc + 1) * 128, scol + c0:scol + c0 + n]),
                              writes=[f"stg{i}"], dma=f"sg{i}")
                        P.add("act", lambda e: e.copy(dst[:, kc, dcol + c0:dcol + c0 + n], stg[i][:, 0:n]),
                              reads=[f"stg{i}"], writes=[key])
                    if now:
                        job()
                    else:
                        b.bg.append(job)

        def bg_flush():
            while b.bg:
                b.bg.pop(0)()
        b.bg_flush = bg_flush

        with contextlib.ExitStack() as es0:
            cp0 = b.tsb("cp0", [128, ncp0], es=es0)
            wm0 = b.tsb("wm0", [128, 8, 3072], BF16, es=es0)
            wal = b.tsb("wal", [128, 8, 16], BF16, es=es0)
            wo0 = b.tsb("wo0", [128, 8, D], BF16, es=es0)
            P.add("sp", lambda e: e.dma_start(out=cp0[:, :], in_=cp0d[:, :]), writes=["cp"], dma="c0")
            wload(wm0, 0, w0d, 0, 1024, "wm0A", now=True)
            wload(wal, 0, wald, 0, 16, "wal", now=True)
            wload(wm0, 1024, w0d, 1024, 2048, "wm0B")
            wload(wo0, 0, wo0d, 0, 1024, "wo")
            wload(wmA1, 0, w1d, 0, 1024, "wm1A")
            b.cp, b.offs = cp0, offs0

            def wmf0(kc, off, n):
                return wm0[:, kc, off:off + n], ("wm0A" if off < 1024 else "wm0B")
            env = dict(xin=xin, wm=wmf0, identb=identb, wal=wal, wo=wo0, flg=flg, psum=psum, xkey=None,
                       st_local=stA, st_local_t=stA_t, st_gath_t=stG_t, st_gath_key="stG")
            NPH = int(_os.environ.get("K_PH", 4))
            if not _os.environ.get("K_SKIP0S"):
                phase(b, "L0S", env)
                P.barrier()
            env.update(yout=x1s, ykey="x1s", sinit=stG[0:64, :], sinit_key="stG")
            bg_flush()
            if NPH >= 2 and not _os.environ.get("K_SKIP0F"):
                phase(b, "L0F", env)
                P.barrier()

        with contextlib.ExitStack() as es1:
            cp1 = b.tsb("cp1", [128, ncp1], es=es1)
            wmB1 = b.tsb("wmB1", [128, 8, 1536], BF16, es=es1)
            wo1 = b.tsb("wo1", [128, 8, D], BF16, es=es1)
            P.add("sp", lambda e: e.dma_start(out=cp1[:, :], in_=cp1d[:, :]), writes=["cp"], dma="c6")
            wload(wmB1, 0, w1d, 1024, 1536, "wm1B")
            wload(wo1, 0, wo1d, 0, 1024, "wo")
            b.cp, b.offs = cp1, offs1

            def wmf1(kc, off, n):
                if off < 1024:
                    return wmA1[:, kc, off:off + n], "wm1A"
                return wmB1[:, kc, off - 1024:off - 1024 + n], "wm1B"
            env = dict(xin=x1s, wm=wmf1, identb=identb, wo=wo1, flg=flg, psum=psum, xkey="x1s", posf=posf,
                       st_local=stB, st_local_t=stB_t, st_gath_t=stH_t, st_gath_key="stH")
            if NPH >= 3 and not _os.environ.get("K_SKIP1S"):
                phase(b, "L1S", env)
                P.barrier()
            env.update(yout=yout, ykey=None, sinit=stH[0:64, :], pprev=stH[64:192, :], sinit_key="stH")
            bg_flush()
            if NPH >= 4:
                phase(b, "L1F", env)
        if P.ops:
            P.emit(nc, b.es)
    return nc, offs0, offs1


_CACHE = {}


def _get():
    if "nc" not in _CACHE:
        _CACHE["nc"] = build()
    return _CACHE["nc"]


def kernel(x, positions, l0_w_in, l0_w_a2, l0_b_a, l0_gla_norm_g, l0_sgu_ln_g, l0_sgu_ln_b,
           l0_w_s, l0_b_s, l0_w_out, l0_ln_g, l0_ln_b, l1_w_in, l1_ret_norm_g, l1_w_pool,
           l1_pool_scale, l1_w_out, l1_ln_g, l1_ln_b):
    f = lambda a: np.ascontiguousarray(np.asarray(a), dtype=np.float32)
    x = f(x)
    positions = np.ascontiguousarray(np.asarray(positions), dtype=np.int32)
    w0 = f(l0_w_in)
    w0m = np.ascontiguousarray(np.concatenate(
        [w0[:, 0:256], w0[:, 256:512], w0[:, 512:1024], w0[:, 1024:1536], w0[:, 1552:2064], w0[:, 2064:2576], w0[:, 2576:3088]], axis=1))
    w0a = np.ascontiguousarray(w0[:, 1536:1552])
    w1m = f(l1_w_in)
    wo0, wo1 = f(l0_w_out), f(l1_w_out)
    cp0, offs0 = l0_pack(f(l0_w_a2), f(l0_b_a), f(l0_gla_norm_g), f(l0_sgu_ln_g), f(l0_sgu_ln_b), f(l0_w_s), f(l0_b_s),
                         f(l0_ln_g), f(l0_ln_b))
    cp1 = []
    for hf in range(2):
        c_, offs1 = l1_pack(f(l1_ret_norm_g), f(l1_w_pool), f(l1_pool_scale), f(l1_ln_g), f(l1_ln_b), hf == 0)
        cp1.append(c_)
    nc, o0, o1 = _get()
    assert o0 == offs0 and o1 == offs1
    in_maps = []
    for c in range(NCORES):
        bi, hf = c // 2, c % 2
        in_maps.append({
            "x": np.ascontiguousarray(x[bi, hf * TOK:(hf + 1) * TOK, :]),
            "cp0": cp0, "cp1": cp1[hf], "w0m": w0m, "w0a": w0a, "wo0": wo0, "w1m": w1m, "wo1": wo1,
            "pos": np.ascontiguousarray(positions[bi, hf * TOK:(hf + 1) * TOK].reshape(NT, 128).T),
            "flag": np.full((128, 1), float(hf), np.float32),
        })
    res = run_bass_kernel_spmd(nc, in_maps, core_ids=list(range(NCORES)))
    out = np.empty((4, 2 * TOK, D), np.float32)
    for c in range(NCORES):
        out[c // 2, (c % 2) * TOK:(c % 2 + 1) * TOK, :] = np.asarray(res.results[c]["y"], np.float32)
    return out
```

```python
import contextlib
import math
import numpy as np
import ml_dtypes
import concourse.bass as bass
import concourse.mybir as mybir
from concourse.bass_utils import run_bass_kernel_spmd

F32 = mybir.dt.float32
BF16 = mybir.dt.bfloat16
I32 = mybir.dt.int32
ALU = mybir.AluOpType
AF = mybir.ActivationFunctionType
AX = mybir.AxisListType

NCORES = 8
D = 1024
TOK = 4096
NT = TOK // 128
import os as _os
NT_RUN = int(_os.environ.get('K_NT', NT))
STOP = int(_os.environ.get('K_STOP', 99))
DBG = int(_os.environ.get('K_DBG', 0))
ALPHA = float(4 ** 0.25)
EPS = 1e-5
GC0 = math.sqrt(2.0 / math.pi)
GC1 = 0.044715 * GC0


class Op:
    pass


class Prog:
    ENGS = ("pe", "act", "dve", "pool", "sp")

    def __init__(self):
        self.ops = []
        self.cnt = {}
        self.semh = {}

    def add(self, eng, fn, reads=(), writes=(), dma=None, inc=16, xdeps=()):
        op = Op()
        op.eng, op.fn, op.reads, op.writes, op.dma = eng, fn, tuple(reads), tuple(writes), dma
        op.inc = inc
        op.xdeps = list(xdeps)
        op.need = False
        op.deps = []
        self.ops.append(op)
        return op

    def capture(self, fn):
        n = len(self.ops)
        fn()
        out = self.ops[n:]
        del self.ops[n:]
        return out

    @staticmethod
    def merge(a, b):
        if _os.environ.get("K_MERGE") == "pe" and a and b:
            wa = [1.0 if o.eng == "pe" else 0.05 for o in a]
            wb = [1.0 if o.eng == "pe" else 0.05 for o in b]
            ta, tb = sum(wa), sum(wb)
            out, i, j, ca, cb = [], 0, 0, 0.0, 0.0
            while i < len(a) or j < len(b):
                if j >= len(b) or (i < len(a) and ca / ta <= cb / tb):
                    out.append(a[i]); ca += wa[i]; i += 1
                else:
                    out.append(b[j]); cb += wb[j]; j += 1
            return out
        out, i, j = [], 0, 0
        while i < len(a) or j < len(b):
            if j >= len(b) or (i < len(a) and i * len(b) <= j * len(a)):
                out.append(a[i]); i += 1
            else:
                out.append(b[j]); j += 1
        return out

    def barrier(self):
        return
        last = {}
        for op in self.ops:
            if op.fn is None:
                continue
            if op.dma is not None and not (op.dma.startswith("os") or op.dma.startswith("of") or op.dma.startswith("cc")):
                continue
            last[op.dma if op.dma is not None else op.eng] = op
        for e in self.ENGS:
            self.add(e, None, xdeps=list(last.values()))

    def plan(self):
        last_w, readers = {}, {}
        for op in self.ops:
            deps = []
            for k in op.reads:
                w = last_w.get(k)
                if w is not None:
                    deps.append((w, "RAW"))
            for k in op.writes:
                w = last_w.get(k)
                if w is not None:
                    deps.append((w, "WAW"))
                for r in readers.get(k, ()):
                    deps.append((r, "WAR"))
            for p in op.xdeps:
                if p not in op.deps:
                    op.deps.append(p)
                    p.need = True
            for p, kind in deps:
                if p is op:
                    continue
                if p.dma is None and op.dma is None and p.eng == op.eng:
                    if op.eng == "pe" or kind != "RAW":
                        continue
                if p not in op.deps:
                    op.deps.append(p)
                    p.need = True
            for k in op.reads:
                readers.setdefault(k, []).append(op)
            for k in op.writes:
                last_w[k] = op
                readers[k] = []
        cnt = self.cnt
        for op in self.ops:
            if op.dma is not None:
                cnt[op.dma] = cnt.get(op.dma, 0) + op.inc
                op.sem, op.val = op.dma, cnt[op.dma]
            elif op.need:
                cnt[op.eng] = cnt.get(op.eng, 0) + 1
                op.sem, op.val = op.eng, cnt[op.eng]
        self.sem_names = sorted(set(cnt.keys()) | set(self.ENGS))

    def emit(self, nc, es):
        self.plan()
        for n in self.sem_names:
            if n not in self.semh:
                self.semh[n] = es.enter_context(nc.semaphore("s_" + n))
        semh = self.semh
        by_eng = {e: [op for op in self.ops if op.eng == e] for e in self.ENGS}

        def run(eh, eng):
            waited = {}
            for op in by_eng[eng]:
                wl = {}
                for p in op.deps:
                    wl[p.sem] = max(wl.get(p.sem, 0), p.val)
                for s, v in wl.items():
                    if waited.get(s, 0) >= v:
                        continue
                    eh.wait_ge(semh[s], v)
                    waited[s] = v
                if op.fn is None:
                    continue
                ins = op.fn(eh)
                if op.dma is not None:
                    ins.then_inc(semh[op.dma], op.inc)
                elif op.need:
                    ins.then_inc(semh[eng], 1)

        with nc.Block() as block:
            @block.tensor
            def _(e):
                run(e, "pe")

            @block.scalar
            def _(e):
                run(e, "act")

            @block.vector
            def _(e):
                run(e, "dve")

            @block.gpsimd
            def _(e):
                run(e, "pool")

            @block.sync
            def _(e):
                run(e, "sp")
        self.ops = []


def _pack(items):
    offs, cols, o = {}, [], 0
    for name, arr in items:
        arr = np.ascontiguousarray(arr, dtype=np.float32).reshape(128, -1)
        offs[name] = (o, arr.shape[1])
        cols.append(arr)
        o += arr.shape[1]
    return np.concatenate(cols, axis=1), offs


def _rep(v):
    v = np.asarray(v, np.float32).reshape(1, -1)
    return np.broadcast_to(v, (128, v.shape[1]))


def l0_pack(w_a2, b_a, gla_g, sgu_g, sgu_b, w_s, b_s, ln_g, ln_b):
    j = np.arange(128)[:, None]
    i = np.arange(128)[None, :]
    same = (j // 64) == (i // 64)
    mincl = np.where(same & (j <= i), -1.0 / 16.0, 0.0)
    mafter = np.where(same & (j > i), -1.0 / 16.0, 0.0)
    cm = np.where(same & (j <= i), 1.0, 0.0)
    wa2 = np.zeros((128, 256), np.float32)
    wa2[0:16] = w_a2
    wa2[16] = b_a
    items = [
        ("ident", np.eye(128)),
        ("mincl", mincl), ("mafter", mafter),
        ("negs", np.stack([np.where(np.arange(128) < 64, -1.0 / 16.0, 0.0), np.where(np.arange(128) >= 64, -1.0 / 16.0, 0.0)], axis=1)),
        ("cmask", np.tile(cm, (1, 4))),
        ("gainA", _rep(gla_g.reshape(-1))),
        ("sguG", _rep(sgu_g.reshape(-1))), ("sguB", _rep(sgu_b.reshape(-1))),
        ("wsT", np.transpose(w_s, (2, 0, 1)).reshape(128, 512)),
        ("trilT", np.tile(np.where(j <= i, 1.0, 0.0), (1, 4))),
        ("bs", np.transpose(b_s, (1, 0))),
        ("lng", _rep(ln_g)), ("lnb", _rep(ln_b)),
        ("wa2", wa2),
        ("neghalf", np.full((128, 4), -0.5)),
        ("ones", np.ones((128, 128))),
    ]
    return _pack(items)


def l1_pack(ret_g, w_pool, pool_scale, ln_g, ln_b, first_half):
    H, C = 4, 128
    lg = np.log(1.0 - 2.0 ** (-5.0 - np.arange(H, dtype=np.float64)))
    idx = np.arange(C, dtype=np.float64)
    half = 32
    inv = 10000.0 ** (-np.arange(half, dtype=np.float64) / half) / (2.0 * np.pi)
    invR = np.tile(inv[None, :], (8, 1)).reshape(-1)
    jj = idx[:, None]
    ii = idx[None, :]
    dm = np.zeros((128, H, 128))
    for h in range(H):
        dm[:, h, :] = np.where(ii >= jj, np.exp(np.maximum(ii - jj, 0.0) * lg[h]), 0.0) / 8.0
    winter = np.exp((idx[:, None] + 1.0) * lg[None, :]) / 8.0
    wstate = np.exp((C - 1.0 - idx)[:, None] * lg[None, :])
    winterR = np.repeat(winter[:, :, None], 64, axis=2).reshape(128, 256)
    wstateR = np.repeat(wstate[:, :, None], 64, axis=2).reshape(128, 256)
    gd = np.exp(C * lg)
    gdec = np.broadcast_to(gd[None, :], (128, 4))
    wins = (2, 4, 8, 16)
    s = np.arange(128)[:, None]
    t = np.arange(128)[None, :]
    gcur = np.zeros((128, 4, 128)); gprev = np.zeros((128, 4, 128)); icn = np.zeros((128, 4, 128))
    gcur0 = np.zeros((128, 4, 128)); gprev0 = np.zeros((128, 4, 128)); icn0 = np.zeros((128, 4, 128))
    for g, w in enumerate(wins):
        ind = ((s <= t) & (s > t - w)).astype(np.float64)
        gcur[:, g, :] = ind - w * (s == t)
        gprev[:, g, :] = ((s - 128) > (t - w)).astype(np.float64)
        icn[:, g, :] = 1.0 / w
        if first_half:
            cntt = np.minimum(t + 1, w).astype(np.float64)
            gcur0[:, g, :] = ind - cntt * (s == t)
            gprev0[:, g, :] = 0.0
            icn0[:, g, :] = np.broadcast_to(1.0 / cntt, (128, 128))
        else:
            gcur0[:, g, :] = gcur[:, g, :]; gprev0[:, g, :] = gprev[:, g, :]; icn0[:, g, :] = icn[:, g, :]
    items = [
        ("ident", np.eye(128)),
        ("invR", _rep(invR)),
        ("dmatT", dm.reshape(128, 512)),
        ("winterR", winterR), ("wstateR", wstateR),
        ("gdec", gdec),
        ("gainC", _rep(ret_g.reshape(-1))),
        ("gcur", gcur.reshape(128, 512)), ("gprev", gprev.reshape(128, 512)), ("icn", icn.reshape(128, 512)),
        ("gcur0", gcur0.reshape(128, 512)), ("gprev0", gprev0.reshape(128, 512)), ("icn0", icn0.reshape(128, 512)),
        ("wpool", np.transpose(w_pool, (1, 0, 2)).reshape(128, 512)),
        ("pscale", _rep(pool_scale)),
        ("lng", _rep(ln_g)), ("lnb", _rep(ln_b)),
        ("neghalf", np.full((128, 4), -0.5)),
    ]
    return _pack(items)


class B:
    def __init__(self):
        self.nc = bass.Bass("TRN2", target_bir_lowering=False)
        self.P = Prog()
        self.es = contextlib.ExitStack()
        self.pes = None
        self.pfx = ""
        self.offs = None
        self.cp = None

    def sb(self, name, shape, dt=F32):
        return self.pes.enter_context(self.nc.sbuf_tensor("sb_" + self.pfx + name, list(shape), dt))

    def tsb(self, name, shape, dt=F32, es=None):
        return (es or self.es).enter_context(self.nc.sbuf_tensor("sb_" + name, list(shape), dt))

    def ps(self, name, shape, dt=F32):
        return self.es.enter_context(self.nc.psum_tensor("ps_" + name, list(shape), dt))

    def dram(self, name, shape, dt, kind="Internal"):
        return self.nc.dram_tensor(name, list(shape), dt, kind=kind)

    def c(self, name, lo=0, hi=None, rows=slice(None)):
        o, n = self.offs[name]
        hi = n if hi is None else hi
        return self.cp[rows, o + lo:o + hi]


def phase(b, mode, env):
    nc, P = b.nc, b.P
    L = 0 if mode.startswith("L0") else 1
    full = mode.endswith("F")
    b.pfx = mode + "_"
    STOPL = int(_os.environ.get(f"K_STOP{L}", STOP))
    cp_, offs_ = b.cp, b.offs

    def c_(name, lo=0, hi=None, rows=slice(None)):
        o, n = offs_[name]
        hi = n if hi is None else hi
        return cp_[rows, o + lo:o + hi]
    xin, yout = env["xin"], env.get("yout")
    wmf, identb = env["wm"], env["identb"]
    wal, wo = env.get("wal"), env.get("wo")
    posf = env.get("posf")
    flg = env["flg"]
    tp, b1, b2, b3, b4, b5, b6, b7 = env["psum"]
    xkey = env["xkey"]
    ykey = env.get("ykey")
    with contextlib.ExitStack() as pes:
        b.pes = pes
        x32 = [b.sb(f"x32_{i}", [128, D]) for i in range(2)]
        xb = [b.sb(f"xb_{i}", [128, D], BF16) for i in range(2)]
        xT = [b.sb(f"xT_{i}", [128, 8, 128], BF16) for i in range(2)]
        vbs = [b.sb(f"vb_{i}", [128, 512], BF16) for i in range(2)]
        S32 = b.sb("S32", [64, 4, 128])
        Sb = [b.sb(f"Sb_{i}", [64, 4, 128], BF16) for i in range(2)]
        kst = b.sb("kst", [128, 256], BF16)
        kstc = [b.sb(f"kstc_{i}", [128, 256], BF16) for i in range(2)]
        if L == 0:
            alT = b.sb("alT", [32, 128])
            e1 = b.sb("e1", [128, 256])
            spl = b.sb("spl", [128, 256])
            Eq = b.sb("Eq", [128, 256]); Ek = b.sb("Ek", [128, 256]); Es = b.sb("Es", [128, 256])
            dec = b.sb("dec", [64, 8])
        else:
            ang = b.sb("ang", [128, 512])
            angi = b.sb("angi", [128, 512], I32)
            angf = b.sb("angf", [128, 512])
            gt1 = b.sb("gt1", [128, 512])
            sc_ = b.sb("sincos", [128, 512])
            rt = [b.sb(f"rt{i}", [128, 256]) for i in range(4)]
            qkr = b.sb("qkr", [128, 512])
            plst = b.sb("plst", [128, 512])
        if full:
            qkd = b.sb("qkd", [128, 768], BF16)
            qkT = b.sb("qkT", [64, 12, 128], BF16)
            qm = [b.sb(f"qm_{i}", [64, 4, 128], BF16) for i in range(2)]
            scT = b.sb("scT", [128, 512], BF16)
            sq = b.sb("sq", [128, 512])
            ss = b.sb("ss", [128, 4]); vv = b.sb("vv", [128, 4]); rstd = b.sb("rstd", [128, 4])
            oa1 = b.sb("oa1", [128, 512]); oa2 = b.sb("oa2", [128, 512])
            sg1 = b.sb("sg1", [128, 512]); sg2 = b.sb("sg2", [128, 512])
            cat = b.sb("cat", [128, D], BF16)
            catT = b.sb("catT", [128, 8, 128], BF16)
            bst = b.sb("bst", [128, 4, 6]); bmv = b.sb("bmv", [128, 4, 2])
            bstS = b.sb("bstS", [128, 4, 6]); bmvS = b.sb("bmvS", [128, 4, 2])
            vvS = b.sb("vvS", [128, 4]); rstdS = b.sb("rstdS", [128, 4])
            r32 = b.sb("r32", [128, D])
            bst2 = b.sb("bst2", [128, 12]); bmv2 = b.sb("bmv2", [128, 2])
            vv2 = b.sb("vv2", [128, 1]); rstd2 = b.sb("rstd2", [128, 1])
            n32 = b.sb("n32", [128, D])
            o32 = [b.sb(f"o32_{i}", [128, D]) for i in range(2)]
            if L == 0:
                wsTb = b.sb("wsTb", [128, 4, 128], BF16)
                wsTm = b.sb("wsTm", [128, 512])
                squ = b.sb("squ", [128, 512]); inn = b.sb("inn", [128, 512])
                gu = b.sb("gu", [128, 512]); gsv = b.sb("gsv", [128, 512])
                nrm = b.sb("nrm", [128, 512]); nrm2 = b.sb("nrm2", [128, 512])
                svn = b.sb("svn", [128, 512], BF16)
                t1 = b.sb("t1", [128, 512])
            else:
                pbuf = [b.sb(f"pb_{i}", [128, 512], BF16) for i in range(2)]
                pinit = b.sb("pinit", [128, 512])
                gcb = [b.sb(f"gcb_{i}", [128, 4, 128], BF16) for i in range(2)]
                gpb = [b.sb(f"gpb_{i}", [128, 4, 128], BF16) for i in range(2)]
                wpb = b.sb("wpb", [128, 4, 128], BF16)
                plT = b.sb("plT", [128, 4, 128], BF16)
                on = b.sb("on", [128, 512])

        KX = _os.environ.get("K_X", "")
        if full and not ("1" in KX and L == 0):
            sini = b.sb("sini", [64, 4, 128])
            P.add("sp", lambda e: e.dma_start(out=sini[:, :, :], in_=env["sinit"].rearrange("p (a b) -> p a b", a=4)),
                  reads=[env["sinit_key"]], writes=["sini"], dma="c1" + mode)
            P.add("dve", lambda e: e.tensor_scalar(S32[:, :, :], sini[:, :, :], flg[0:64, 0:1], None, ALU.mult),
                  reads=["sini", "flg"], writes=["S32"])
        else:
            P.add("dve", lambda e: e.memset(S32[:, :, :], 0.0), writes=["S32"])
        P.add("act", lambda e: e.copy(Sb[0][:, :, :], S32[:, :, :]), reads=["S32"], writes=["Sb0"])
        if L == 0:
            P.add("dve", lambda e: e.memset(alT[:, :], 1.0), writes=["alT"])
            for i in range(2):
                if "2" in KX:
                    continue
                P.add("dve", lambda e, i=i: e.memset(kstc[i][:, :], 0.0), writes=[f"kstc{i}"])
                if full:
                    P.add("dve", lambda e, i=i: e.memset(qm[i][:, :, :], 0.0), writes=[f"qm{i}"])
            if full and "3" not in KX:
                P.add("dve", lambda e: e.tensor_tensor(wsTm[:, :], c_("wsT"), c_("trilT"), ALU.mult),
                      reads=["cp"], writes=["wsTm"])
                P.add("dve", lambda e: e.tensor_copy(wsTb[:, :, :], wsTm[:, :].rearrange("p (g t) -> p g t", g=4)),
                      reads=["wsTm"], writes=["wsTb"])
        else:
            if full:
                P.add("sp", lambda e: e.dma_start(out=pinit[:, :], in_=env["pprev"]), reads=[env["sinit_key"]], writes=["pinit"], dma="c3")
                P.add("dve", lambda e: e.tensor_scalar(pbuf[1][:, :], pinit[:, :], flg[:, 0:1], None, ALU.mult),
                      reads=["pinit", "flg"], writes=["pb1"])
                for i, (gc_, gp_) in enumerate((("gcur0", "gprev0"), ("gcur", "gprev"))):
                    P.add("dve", lambda e, i=i, gc_=gc_: e.tensor_copy(
                        gcb[i][:, :, :], c_(gc_).rearrange("p (g t) -> p g t", g=4)), reads=["cp"], writes=[f"gcb{i}"])
                    P.add("dve", lambda e, i=i, gp_=gp_: e.tensor_copy(
                        gpb[i][:, :, :], c_(gp_).rearrange("p (g t) -> p g t", g=4)), reads=["cp"], writes=[f"gpb{i}"])
                P.add("dve", lambda e: e.tensor_copy(wpb[:, :, :], c_("wpool").rearrange("p (g t) -> p g t", g=4)),
                      reads=["cp"], writes=["wpb"])

        def proj(sl, off, n, out_ap, key, extra_reads=()):
            for kc in range(8):
                wap, wkey = wmf(kc, off, n)
                P.add("pe", lambda e, kc=kc, wap=wap: e.matmul(out_ap, xT[sl][:, kc, :], wap,
                                                               start=(kc == 0), stop=(kc == 7)),
                      reads=[f"xT{'a' if kc < 4 else 'b'}{sl}", wkey] + list(extra_reads), writes=[key])

        def load_x(t):
            sl = t % 2
            P.add("sp", lambda e: e.dma_start(out=x32[sl][:, :], in_=xin[t * 128:(t + 1) * 128, :]),
                  reads=([f"{xkey}{t}"] if xkey else []), writes=[f"x32_{sl}"], dma=f"xs{sl}")

        def tile(t, part="all"):
            sl = t % 2
            doH, doG, doS, doT = (part in ("all", x) for x in "HGST")
            if part == "all":
                tpx, tpk = tp, "tp"
            elif full:
                tpx, tpk = b3[:, :].bitcast(BF16).rearrange("p (a c) -> p a c", a=8), "b3"
            else:
                tpx, tpk = b5[:, :].bitcast(BF16).rearrange("p (a c) -> p a c", a=8), "b5"
            if STOPL <= 0:
                return
            early_g1 = full and part != "all" and not _os.environ.get("K_LATEG1")
            if doH:
              head_part(t, sl, tpx, tpk)
              if early_g1:
                g1_part(t, sl)
            if doG:
              if not (early_g1 and part == "G"):
                g1_part(t, sl)
              g2_part(t, sl)
            if full and doS:
              s_part(t, sl)
            if full and doT:
              tail_part(t, sl)

        def pbanks(t):
            if full or t % 2 == 0 or _os.environ.get("K_NOTHREAD"):
                return b1, b2, "b1", "b2"
            return b6, b7, "b6", "b7"

        def head_part(t, sl, tpx, tpk):
            for _ in range(3):
                if b.bg:
                    b.bg.pop(0)()
            if tpk == "tp" and t + 1 < NT_RUN:
                load_x(t + 1)
            P.add("dve", lambda e, sl=sl: e.tensor_copy(xb[sl][:, :], x32[sl][:, :]),
                  reads=[f"x32_{sl}"], writes=[f"xb{sl}"])
            for kc in range(8):
                P.add("pe", lambda e, kc=kc, sl=sl: e.transpose(tpx[:, kc, :], xb[sl][:, kc * 128:(kc + 1) * 128], identb[:, :]),
                      reads=[f"xb{sl}", "identb"], writes=[tpk])
            P.add("act", lambda e, sl=sl: e.copy(xT[sl][:, 0:4, :], tpx[:, 0:4, :]), reads=[tpk], writes=[f"xTa{sl}"])
            P.add("act", lambda e, sl=sl: e.copy(xT[sl][:, 4:8, :], tpx[:, 4:8, :]), reads=[tpk], writes=[f"xTb{sl}"])

            if STOPL <= 1:
                return
            bq, bv, bqk, bvk = pbanks(t)
            if full:
                proj(sl, 0, 512, bq[:, :], bqk)
            else:
                proj(sl, 256, 256, bq[:, 256:512], bqk)
            proj(sl, 512, 512, bv[:, :], bvk)
            P.add("act", lambda e: e.copy(vbs[sl][:, :], bv[:, :]), reads=[bvk], writes=[f"vb{sl}"])
            if full:
                proj(sl, 1024, 512, b6[:, :], "b6")
                proj(sl, 1536, 512, b7[:, :], "b7")
                if L == 0:
                    P.add("act", lambda e: e.activation(sg1[:, :], b6[:, :], AF.Sigmoid), reads=["b6"], writes=["sg1"])
                    P.add("dve", lambda e: e.tensor_tensor(sg2[:, :], b6[:, :], sg1[:, :], ALU.mult), reads=["b6", "sg1"], writes=["sg2"])
                    P.add("pool", lambda e: e.tensor_tensor(oa2[:, :], sg2[:, :], c_("gainA"), ALU.mult), reads=["sg2", "cp"], writes=["gs"])
                else:
                    P.add("act", lambda e: e.activation(sg2[:, :], b6[:, :], AF.Silu), reads=["b6"], writes=["sg2"])
                    P.add("pool", lambda e: e.tensor_tensor(oa2[:, :], sg2[:, :], c_("gainC"), ALU.mult), reads=["sg2", "cp"], writes=["gs"])

        def g1_part(t, sl):
            bq, bv, bqk, bvk = pbanks(t)
            if STOPL <= 2:
                return
            if L == 0:
                for kc in range(8):
                    P.add("pe", lambda e, kc=kc, sl=sl: e.matmul(b3[0:16, 256:384], wal[:, kc, :], xT[sl][:, kc, :],
                                                                 start=(kc == 0), stop=(kc == 7)),
                          reads=[f"xT{'a' if kc < 4 else 'b'}{sl}", "wal"], writes=["b3"])
                P.add("dve", lambda e: e.tensor_copy(alT[0:16, :], b3[0:16, 256:384]), reads=["b3"], writes=["alT"])
                P.add("pe", lambda e: e.matmul(b3[:, 0:256], alT[0:17, :], c_("wa2", rows=slice(0, 17)), start=True, stop=True),
                      reads=["alT", "cp"], writes=["b3"])
                P.add("act", lambda e: e.activation(e1[:, :], b3[:, 0:256], AF.Exp, scale=-1.0), reads=["b3"], writes=["e1"])
                P.add("act", lambda e: e.activation(spl[:, :], e1[:, :], AF.Ln, bias=1.0), reads=["e1"], writes=["spl"])
                P.add("pe", lambda e: e.matmul(b4[:, 0:256], c_("mincl"), spl[:, :], start=True, stop=True),
                      reads=["spl", "cp"], writes=["b4"])
                P.add("pe", lambda e: e.matmul(b4[:, 256:512], c_("mafter"), spl[:, :], start=True, stop=True),
                      reads=["spl", "cp"], writes=["b4"])
                for h in range(4):
                    P.add("pe", lambda e, h=h: e.matmul(
                        b3[0:64, 384 + h * 2:386 + h * 2], spl[:, h * 64:(h + 1) * 64],
                        c_("negs", 0, 2), start=True, stop=True),
                        reads=["spl", "cp"], writes=["b3"])
                if full:
                    P.add("act", lambda e: e.activation(Eq[:, :], b4[:, 0:256], AF.Exp), reads=["b4"], writes=["Eq"])
                    P.add("act", lambda e: e.activation(Ek[:, :], b4[:, 0:256], AF.Exp, scale=-1.0), reads=["b4"], writes=["Ek"])
                P.add("act", lambda e: e.activation(Es[:, :], b4[:, 256:512], AF.Exp), reads=["b4"], writes=["Es"])
                P.add("act", lambda e: e.activation(dec[:, :], b3[0:64, 384:392], AF.Exp), reads=["b3"], writes=["dec"])
                if full:
                    P.add("dve", lambda e: e.scalar_tensor_tensor(qkd[:, 0:256], b1[:, 0:256], 0.125, Eq[:, :], ALU.mult, ALU.mult),
                          reads=["b1", "Eq"], writes=["qkd"])
                    P.add("dve", lambda e: e.tensor_tensor(qkd[:, 256:512], b1[:, 256:512], Ek[:, :], ALU.mult),
                          reads=["b1", "Ek"], writes=["qkd"])
                for c in range(2):
                    P.add("dve", lambda e, c=c: e.tensor_tensor(kstc[c][64 * c:64 * c + 64, :], bq[64 * c:64 * c + 64, 256:512],
                                                                Es[64 * c:64 * c + 64, :], ALU.mult),
                          reads=[bqk, "Es"], writes=[f"kstc{c}"])
                nblk = 8
            else:
                if t % 8 == 0:
                    n8 = min(8, NT_RUN - t)
                    w8 = n8 * 64
                    angv = ang[:, 0:w8].rearrange("p (j s f) -> p j s f", s=2, f=32)
                    P.add("dve", lambda e: e.tensor_tensor(
                        angv[:, :, 0, :], posf[:, t:t + n8].unsqueeze(2).broadcast_to([128, n8, 32]),
                        c_("invR", 0, 32).unsqueeze(1).broadcast_to([128, n8, 32]), ALU.mult),
                        reads=["cp", "posf"], writes=["ang"])
                    P.add("dve", lambda e: e.tensor_scalar(angv[:, :, 1, :], angv[:, :, 0, :], 0.25, None, ALU.add),
                          reads=["ang"], writes=["ang"])
                    P.add("dve", lambda e: e.tensor_copy(angi[:, 0:w8], ang[:, 0:w8]), reads=["ang"], writes=["angi"])
                    P.add("dve", lambda e: e.tensor_copy(angf[:, 0:w8], angi[:, 0:w8]), reads=["angi"], writes=["angf"])
                    P.add("dve", lambda e: e.tensor_tensor(ang[:, 0:w8], ang[:, 0:w8], angf[:, 0:w8], ALU.subtract),
                          reads=["ang", "angf"], writes=["ang"])
                    P.add("dve", lambda e: e.tensor_scalar(gt1[:, 0:w8], ang[:, 0:w8], 0.5, None, ALU.is_gt), reads=["ang"], writes=["gt1"])
                    P.add("dve", lambda e: e.tensor_tensor(ang[:, 0:w8], ang[:, 0:w8], gt1[:, 0:w8], ALU.subtract),
                          reads=["ang", "gt1"], writes=["ang"])
                    P.add("dve", lambda e: e.tensor_scalar(gt1[:, 0:w8], ang[:, 0:w8], -0.5, None, ALU.is_lt), reads=["ang"], writes=["gt1"])
                    P.add("dve", lambda e: e.tensor_tensor(ang[:, 0:w8], ang[:, 0:w8], gt1[:, 0:w8], ALU.add),
                          reads=["ang", "gt1"], writes=["ang"])
                    P.add("act", lambda e: e.activation(sc_[:, 0:w8], ang[:, 0:w8], AF.Sin, scale=2.0 * math.pi), reads=["ang"], writes=["sincos"])
                lo = 0 if full else 4
                nh = 8 - lo

                def v4(ap):
                    return ap.rearrange("p (a two c) -> p a two c", two=2, c=32)
                hq = v4(bq[:, :])
                j8 = (t % 8) * 64
                sinv = sc_[:, j8:j8 + 32].unsqueeze(1).broadcast_to([128, 8, 32])
                cosv = sc_[:, j8 + 32:j8 + 64].unsqueeze(1).broadcast_to([128, 8, 32])
                rtv = [r[:, :].rearrange("p (a c) -> p a c", c=32) for r in rt]
                qv = v4(qkr[:, :])
                for i_, (half_, tab) in enumerate(((0, cosv), (1, sinv), (0, sinv), (1, cosv))):
                    P.add("dve", lambda e, i_=i_, half_=half_, tab=tab: e.tensor_tensor(
                        rtv[i_][:, lo:8, :], hq[:, lo:8, half_, :], tab[:, lo:8, :], ALU.mult),
                        reads=[bqk, "sincos"], writes=[f"rt{i_}"])
                P.add("dve", lambda e: e.tensor_tensor(qv[:, lo:8, 0, :], rtv[0][:, lo:8, :], rtv[1][:, lo:8, :], ALU.subtract),
                      reads=["rt0", "rt1"], writes=["qkr"])
                P.add("dve", lambda e: e.tensor_tensor(qv[:, lo:8, 1, :], rtv[2][:, lo:8, :], rtv[3][:, lo:8, :], ALU.add),
                      reads=["rt2", "rt3"], writes=["qkr"])
                P.add("pool", lambda e: e.tensor_tensor(kst[:, :], qkr[:, 256:512], c_("wstateR"), ALU.mult),
                      reads=["qkr", "cp"], writes=["kst"])
                if full:
                    P.add("act", lambda e: e.copy(qkd[:, 0:512], qkr[:, :]), reads=["qkr"], writes=["qkd"])
                    P.add("dve", lambda e: e.tensor_tensor(qkd[:, 512:768], qkr[:, 0:256], c_("winterR"), ALU.mult),
                          reads=["qkr", "cp"], writes=["qkd"])
                nblk = 12

        def g2_part(t, sl):
            bq, bv, bqk, bvk = pbanks(t)
            nblk = 8 if L == 0 else 12
            if STOPL <= 3:
                return
            def l0_U(c):
                for h in range(4):
                    P.add("pe", lambda e, h=h, c=c: e.matmul(
                        b4[0:64, h * 128:(h + 1) * 128], kstc[c][:, h * 64:(h + 1) * 64], vbs[sl][:, h * 128:(h + 1) * 128],
                        start=True, stop=True), reads=[f"kstc{c}", f"vb{sl}"], writes=["b4"])

            def l0_update(c):
                for h in range(4):
                    P.add("dve", lambda e, c=c, h=h: e.scalar_tensor_tensor(
                        S32[:, h, :], S32[:, h, :], dec[:, h * 2 + c:h * 2 + c + 1],
                        b4[0:64, h * 128:(h + 1) * 128], ALU.mult, ALU.add),
                        reads=["S32", "dec", "b4"], writes=["S32"])
                nxt = 1 - c
                if full:
                    P.add("act", lambda e, nxt=nxt: e.copy(Sb[nxt][:, :, :], S32[:, :, :]), reads=["S32"], writes=[f"Sb{nxt}"])

            if L == 0:
                l0_U(0)
                if STOPL <= 4:
                    return
                l0_update(0)
                if STOPL <= 5:
                    return
            if full:
                for blk in range(8):
                    P.add("pe", lambda e, blk=blk: e.transpose(tp[0:64, blk, :], qkd[:, blk * 64:(blk + 1) * 64], identb[:, :]),
                          reads=["qkd", "identb"], writes=["tp"])
                P.add("act", lambda e: e.copy(qkT[:, 0:8, :], tp[0:64, 0:8, :]), reads=["tp"], writes=["qkT"])
                if L == 0:
                    for c in range(2):
                        P.add("act", lambda e, c=c: e.copy(qm[c][:, :, 64 * c:64 * c + 64], tp[0:64, 0:4, 64 * c:64 * c + 64]),
                              reads=["tp"], writes=[f"qm{c}"])
                for h in range(4):
                    P.add("pe", lambda e, h=h: e.matmul(
                        b5[:, h * 128:(h + 1) * 128], qkT[:, 4 + h, :], qkT[:, h, :],
                        start=True, stop=True), reads=["qkT"], writes=["b5"])
                if L == 1:
                    for blk in range(4):
                        P.add("pe", lambda e, blk=blk: e.transpose(tp[0:64, blk, :], qkd[:, 512 + blk * 64:512 + (blk + 1) * 64], identb[:, :]),
                              reads=["qkd", "identb"], writes=["tp"])
                    P.add("act", lambda e: e.copy(qkT[:, 8:12, :], tp[0:64, 0:4, :]), reads=["tp"], writes=["qkT2"])
                mk = "cmask" if L == 0 else "dmatT"
                P.add("dve", lambda e: e.tensor_tensor(scT[:, :], b5[:, :], c_(mk), ALU.mult), reads=["b5", "cp"], writes=["scT"])
                for h in range(4):
                    P.add("pe", lambda e, h=h: e.matmul(b1[:, h * 128:(h + 1) * 128], scT[:, h * 128:(h + 1) * 128],
                                                        vbs[sl][:, h * 128:(h + 1) * 128], start=True, stop=False),
                          reads=["scT", f"vb{sl}"], writes=["b1"])
                    if L == 0:
                        for c in range(2):
                            P.add("pe", lambda e, h=h, c=c: e.matmul(
                                b1[:, h * 128:(h + 1) * 128], qm[c][:, h, :], Sb[c][:, h, :], start=False, stop=(c == 1)),
                                reads=[f"qm{c}", f"Sb{c}"], writes=["b1"])
                    else:
                        P.add("pe", lambda e, h=h: e.matmul(
                            b1[:, h * 128:(h + 1) * 128], qkT[:, 8 + h, :], Sb[0][:, h, :],
                            start=False, stop=True), reads=["qkT2", "Sb0"], writes=["b1"])

            if L == 0:
                l0_U(1)
                l0_update(1)
            else:
                for h in range(4):
                    P.add("pe", lambda e, h=h: e.matmul(
                        b4[0:64, h * 128:(h + 1) * 128], kst[:, h * 64:(h + 1) * 64],
                        vbs[sl][:, h * 128:(h + 1) * 128], start=True, stop=True), reads=["kst", f"vb{sl}"], writes=["b4"])
                for h in range(4):
                    P.add("dve", lambda e, h=h: e.scalar_tensor_tensor(
                        S32[:, h, :], S32[:, h, :], c_("gdec", h, h + 1, rows=slice(0, 64)), b4[0:64, h * 128:(h + 1) * 128], ALU.mult, ALU.add),
                        reads=["S32", "cp", "b4"], writes=["S32"])
                if full:
                    P.add("act", lambda e: e.copy(Sb[0][:, :, :], S32[:, :, :]), reads=["S32"], writes=["Sb0"])
                if not full and t == NT_RUN - 1:
                    proj(sl, 1536, 512, b6[:, :], "b6")
                    P.add("act", lambda e: e.copy(plst[:, :], b6[:, :]), reads=["b6"], writes=["plst"])

            if not full:
                return

            if L == 0:
                P.add("act", lambda e: e.activation(sq[:, :], b1[:, :], AF.Square), reads=["b1"], writes=["sq"])
                P.add("dve", lambda e: e.tensor_reduce(ss[:, :], sq[:, :].rearrange("p (h c) -> p h c", h=4), AX.X, ALU.add),
                      reads=["sq"], writes=["ss"])
                P.add("pool", lambda e: e.tensor_scalar(vv[:, :], ss[:, :], 1.0 / 128.0, EPS, ALU.mult, ALU.add), reads=["ss"], writes=["vv"])
                P.add("pool", lambda e: e.tensor_tensor(rstd[:, :], vv[:, :], c_("neghalf"), ALU.pow), reads=["vv", "cp"], writes=["rstd"])
                for h in range(4):
                    P.add("dve", lambda e, h=h: e.scalar_tensor_tensor(
                        cat[:, h * 128:(h + 1) * 128], b1[:, h * 128:(h + 1) * 128], rstd[:, h:h + 1],
                        oa2[:, h * 128:(h + 1) * 128], ALU.mult, ALU.mult), reads=["b1", "rstd", "gs"], writes=["catA"])
            else:
                for h in range(4):
                    P.add("dve", lambda e, h=h: e.bn_stats(bst[:, h, :], b1[:, h * 128:(h + 1) * 128]), reads=["b1"], writes=["bst"])
                for h in range(4):
                    P.add("dve", lambda e, h=h: e.bn_aggr(bmv[:, h, :], bst[:, h, :]), reads=["bst"], writes=["bmv"])
                P.add("pool", lambda e: e.tensor_scalar(vv[:, :], bmv[:, :, 1], EPS, None, ALU.add), reads=["bmv"], writes=["vv"])
                P.add("pool", lambda e: e.tensor_tensor(rstd[:, :], vv[:, :], c_("neghalf"), ALU.pow), reads=["vv", "cp"], writes=["rstd"])
                for h in range(4):
                    P.add("dve", lambda e, h=h: e.tensor_scalar(on[:, h * 128:(h + 1) * 128], b1[:, h * 128:(h + 1) * 128],
                                                                bmv[:, h, 0:1], rstd[:, h:h + 1], ALU.subtract, ALU.mult),
                          reads=["b1", "bmv", "rstd"], writes=["on"])
                P.add("dve", lambda e: e.tensor_tensor(cat[:, 0:512], on[:, :], oa2[:, :], ALU.mult), reads=["on", "gs"], writes=["catA"])

        def s_part(t, sl):
            if L == 0:
                P.add("act", lambda e: e.activation(squ[:, :], b7[:, :], AF.Square, scale=math.sqrt(GC1)), reads=["b7"], writes=["squ"])
                P.add("dve", lambda e: e.scalar_tensor_tensor(inn[:, :], squ[:, :], GC0, b7[:, :], ALU.add, ALU.mult),
                      reads=["squ", "b7"], writes=["inn"])
                P.add("act", lambda e: e.activation(sg1[:, :], inn[:, :], AF.Sigmoid, scale=2.0), reads=["inn"], writes=["sg1"])
                P.add("dve", lambda e: e.tensor_tensor(gu[:, :], b7[:, :], sg1[:, :], ALU.mult), reads=["b7", "sg1"], writes=["gu"])
                proj(sl, 2048, 512, b6[:, :], "b6")
                P.add("act", lambda e: e.activation(squ[:, :], b6[:, :], AF.Square, scale=math.sqrt(GC1)), reads=["b6"], writes=["squ"])
                P.add("dve", lambda e: e.scalar_tensor_tensor(inn[:, :], squ[:, :], GC0, b6[:, :], ALU.add, ALU.mult),
                      reads=["squ", "b6"], writes=["inn"])
                P.add("act", lambda e: e.activation(sg1[:, :], inn[:, :], AF.Sigmoid, scale=2.0), reads=["inn"], writes=["sg1"])
                P.add("dve", lambda e: e.tensor_tensor(gsv[:, :], b6[:, :], sg1[:, :], ALU.mult), reads=["b6", "sg1"], writes=["gsv"])
                for g in range(4):
                    P.add("dve", lambda e, g=g: e.bn_stats(bstS[:, g, :], gsv[:, g * 128:(g + 1) * 128]), reads=["gsv"], writes=["bstS"])
                for g in range(4):
                    P.add("dve", lambda e, g=g: e.bn_aggr(bmvS[:, g, :], bstS[:, g, :]), reads=["bstS"], writes=["bmvS"])
                P.add("pool", lambda e: e.tensor_scalar(vvS[:, :], bmvS[:, :, 1], EPS, None, ALU.add), reads=["bmvS"], writes=["vvS"])
                P.add("pool", lambda e: e.tensor_tensor(rstdS[:, :], vvS[:, :], c_("neghalf"), ALU.pow), reads=["vvS", "cp"], writes=["rstdS"])
                for g in range(4):
                    P.add("dve", lambda e, g=g: e.tensor_scalar(nrm[:, g * 128:(g + 1) * 128], gsv[:, g * 128:(g + 1) * 128],
                                                                bmvS[:, g, 0:1], rstdS[:, g:g + 1], ALU.subtract, ALU.mult),
                          reads=["gsv", "bmvS", "rstdS"], writes=["nrm"])
                P.add("dve", lambda e: e.tensor_tensor(nrm2[:, :], nrm[:, :], c_("sguG"), ALU.mult), reads=["nrm", "cp"], writes=["nrm2"])
                P.add("dve", lambda e: e.tensor_tensor(svn[:, :], nrm2[:, :], c_("sguB"), ALU.add), reads=["nrm2", "cp"], writes=["svn"])
                for g in range(4):
                    P.add("pe", lambda e, g=g: e.matmul(b2[:, g * 128:(g + 1) * 128], wsTb[:, g, :], svn[:, g * 128:(g + 1) * 128],
                                                        start=True, stop=True), reads=["wsTb", "svn"], writes=["b2"])
                for g in range(4):
                    P.add("dve", lambda e, g=g: e.scalar_tensor_tensor(
                        t1[:, g * 128:(g + 1) * 128], b2[:, g * 128:(g + 1) * 128], c_("bs", g, g + 1),
                        gu[:, g * 128:(g + 1) * 128], ALU.add, ALU.mult), reads=["b2", "cp", "gu"], writes=["t1"])
                proj(sl, 2560, 512, b7[:, :], "b7")
                P.add("act", lambda e: e.activation(sg1[:, :], b7[:, :], AF.Sigmoid), reads=["b7"], writes=["sg1"])
                P.add("dve", lambda e: e.tensor_tensor(sg2[:, :], b7[:, :], sg1[:, :], ALU.mult), reads=["b7", "sg1"], writes=["sg2"])
                P.add("dve", lambda e: e.tensor_tensor(cat[:, 512:1024], t1[:, :], sg2[:, :], ALU.mult), reads=["t1", "sg2"], writes=["catB"])
            else:
                cur = t % 2
                P.add("act", lambda e, cur=cur: e.copy(pbuf[cur][:, :], b7[:, :]), reads=["b7"], writes=[f"pb{cur}"])
                ti = 0 if t == 0 else 1
                for g in range(4):
                    P.add("pe", lambda e, g=g, cur=cur, ti=ti: e.matmul(b3[:, g * 128:(g + 1) * 128], pbuf[cur][:, g * 128:(g + 1) * 128],
                                                                        gcb[ti][:, g, :], start=True, stop=False),
                          reads=[f"pb{cur}", f"gcb{ti}"], writes=["b3"])
                    P.add("pe", lambda e, g=g, cur=cur, ti=ti: e.matmul(b3[:, g * 128:(g + 1) * 128], pbuf[1 - cur][:, g * 128:(g + 1) * 128],
                                                                        gpb[ti][:, g, :], start=False, stop=True),
                          reads=[f"pb{1 - cur}", f"gpb{ti}"], writes=["b3"])
                ik = "icn0" if t == 0 else "icn"
                P.add("dve", lambda e, ik=ik: e.tensor_tensor(plT[:, :, :], b3[:, :].rearrange("p (g t) -> p g t", g=4),
                                                              c_(ik).rearrange("p (g t) -> p g t", g=4), ALU.mult),
                      reads=["b3", "cp"], writes=["plT"])
                for g in range(4):
                    P.add("pe", lambda e, g=g: e.matmul(b2[:, g * 128:(g + 1) * 128], plT[:, g, :], wpb[:, g, :], start=True, stop=True),
                          reads=["plT", "wpb"], writes=["b2"])
                proj(sl, 2048, 512, b6[:, :], "b6")
                P.add("act", lambda e: e.activation(sg1[:, :], b6[:, :], (AF.Sigmoid if _os.environ.get("K_NOSILU") else AF.Silu)), reads=["b6"], writes=["sg1"])
                P.add("dve", lambda e: e.tensor_tensor(oa1[:, :], b2[:, :], c_("pscale"), ALU.mult), reads=["b2", "cp"], writes=["oa1"])
                P.add("dve", lambda e: e.tensor_tensor(cat[:, 512:1024], oa1[:, :], sg1[:, :], ALU.mult), reads=["oa1", "sg1"], writes=["catB"])

        def tail_part(t, sl):
            yk = ("b5", "tp")
            ybank = (b5, tp[:, :, :].rearrange("p a c -> p (a c)").bitcast(F32))
            for kc in range(8):
                P.add("pe", lambda e, kc=kc: e.transpose(tp[:, kc, :], cat[:, kc * 128:(kc + 1) * 128], identb[:, :]),
                      reads=["catA", "catB", "identb"], writes=["tp"])
            P.add("act", lambda e: e.copy(catT[:, 0:4, :], tp[:, 0:4, :]), reads=["tp"], writes=["catTa"])
            P.add("act", lambda e: e.copy(catT[:, 4:8, :], tp[:, 4:8, :]), reads=["tp"], writes=["catTb"])
            for nb in range(2):
                for kc in range(8):
                    P.add("pe", lambda e, kc=kc, nb=nb: e.matmul(ybank[nb][:, :], catT[:, kc, :], wo[:, kc, nb * 512:(nb + 1) * 512],
                                                                 start=(kc == 0), stop=(kc == 7)),
                          reads=["catTa" if kc < 4 else "catTb", "wo"], writes=[yk[nb]])
                P.add("dve", lambda e, nb=nb, sl=sl: e.scalar_tensor_tensor(
                    r32[:, nb * 512:(nb + 1) * 512], x32[sl][:, nb * 512:(nb + 1) * 512], ALPHA, ybank[nb][:, :], ALU.mult, ALU.add),
                    reads=[f"x32_{sl}", yk[nb]], writes=[f"r32_{nb}"])
                P.add("dve", lambda e, nb=nb: e.bn_stats(bst2[:, nb * 6:(nb + 1) * 6], r32[:, nb * 512:(nb + 1) * 512]), reads=[f"r32_{nb}"], writes=["bst2"])
            P.add("dve", lambda e: e.bn_aggr(bmv2[:, :], bst2[:, :]), reads=["bst2"], writes=["bmv2"])
            P.add("pool", lambda e: e.tensor_scalar(vv2[:, :], bmv2[:, 1:2], EPS, None, ALU.add), reads=["bmv2"], writes=["vv2"])
            P.add("pool", lambda e: e.tensor_tensor(rstd2[:, :], vv2[:, :], c_("neghalf", 0, 1), ALU.pow), reads=["vv2", "cp"], writes=["rstd2"])
            P.add("dve", lambda e: e.scalar_tensor_tensor(n32[:, :], r32[:, :], bmv2[:, 0:1], c_("lng"), ALU.subtract, ALU.mult),
                  reads=["r32_0", "r32_1", "bmv2", "cp"], writes=["n32"])
            P.add("dve", lambda e, sl=sl: e.scalar_tensor_tensor(o32[sl][:, :], n32[:, :], rstd2[:, 0:1], c_("lnb"), ALU.mult, ALU.add),
                  reads=["n32", "rstd2", "cp"], writes=[f"o32_{sl}"])
            P.add("sp", lambda e, sl=sl, t=t: e.dma_start(out=yout[t * 128:(t + 1) * 128, :], in_=o32[sl][:, :]),
                  reads=[f"o32_{sl}"], writes=[f"yout{sl}"] + ([f"{ykey}{t}"] if ykey else []), dma=f"os{sl}")

        if STOPL > 0:
            load_x(0)
        if _os.environ.get("K_NOTHREAD"):
            for t_ in range(NT_RUN):
                tile(t_)
        elif not full:
            if NT_RUN > 1:
                load_x(1)
            P.ops.extend(P.capture(lambda: tile(0, "H")))
            for t_ in range(NT_RUN):
                g_ = P.capture(lambda: tile(t_, "G"))
                hd_ = P.capture(lambda: tile(t_ + 1, "H")) if t_ + 1 < NT_RUN else []
                ngs_ = int(len(g_) * float(_os.environ.get("K_SGF", 1.0)))
                nhs_ = int(len(hd_) * float(_os.environ.get("K_SHF", 0.5)))
                P.ops.extend(Prog.merge(g_[:ngs_], hd_[:nhs_]))
                P.ops.extend(hd_[nhs_:])
                P.ops.extend(g_[ngs_:])
                if t_ + 2 < NT_RUN:
                    load_x(t_ + 2)
        else:
            if NT_RUN > 1:
                load_x(1)
            P.ops.extend(P.capture(lambda: tile(0, "H")))
            for t_ in range(NT_RUN):
                g_ = P.capture(lambda: tile(t_, "G"))
                s_ = P.capture(lambda: tile(t_, "S"))
                ng_ = int(len(g_) * float(_os.environ.get("K_GF", 1.0)))
                ns_ = int(len(s_) * float(_os.environ.get("K_SF", 0.85)))
                P.ops.extend(Prog.merge(g_[:ng_], s_[:ns_]))
                P.ops.extend(g_[ng_:])
                P.ops.extend(s_[ns_:])
                tl_ = P.capture(lambda: tile(t_, "T"))
                hd_ = P.capture(lambda: tile(t_ + 1, "H")) if t_ + 1 < NT_RUN else []
                nh_ = int(len(hd_) * float(_os.environ.get("K_TF", 0.2)))
                P.ops.extend(Prog.merge(tl_, hd_[:nh_]))
                P.ops.extend(hd_[nh_:])
                if t_ + 2 < NT_RUN:
                    load_x(t_ + 2)

        if full:
            P.add("sp", None, reads=["yout0", "yout1"])
        else:
            st = env["st_local"]
            P.add("sp", lambda e: e.dma_start(out=st[0:64, :].rearrange("p (a b) -> p a b", a=4), in_=S32[:, :, :]),
                  reads=["S32"], writes=["st_local"], dma="of")
            if L == 1:
                P.add("sp", lambda e: e.dma_start(out=st[64:192, :], in_=plst[:, :]), reads=["plst"], writes=["st_local"], dma="of")
            if _os.environ.get("K_NOCC"):
                P.add("sp", lambda e: e.dma_start(out=env["st_gath_t"].ap()[0:(64 if L == 0 else 192), :], in_=env["st_local_t"].ap()),
                      reads=["st_local"], writes=[env["st_gath_key"]], dma="of2")
            else:
              P.add("pool", lambda e: e.collective_compute(
                "AllGather", ALU.bypass, replica_groups=[[0, 1], [2, 3], [4, 5], [6, 7]],
                ins=[env["st_local_t"].ap().opt()], outs=[env["st_gath_t"].ap().opt()]),
                reads=["st_local"], writes=[env["st_gath_key"]], dma=f"cc{L}", inc=1)
        last = {}
        for op in P.ops:
            if op.fn is not None and op.dma is not None and not (op.dma.startswith("sg") or op.dma.startswith("xs")):
                last[op.dma] = op
        P.add("sp", None, xdeps=list(last.values()))
        P.emit(nc, b.es)
        b.pes = None


def build():
    b = B()
    nc, P = b.nc, b.P
    cp0_, offs0 = l0_pack(*[np.zeros(s_, np.float32) for s_ in ((16, 256), (256,), (4, 128), (4, 128), (4, 128), (4, 128, 128), (4, 128), (1024,), (1024,))])
    cp1_, offs1 = l1_pack(np.zeros((4, 128), np.float32), np.zeros((4, 128, 128), np.float32), np.zeros((512,), np.float32),
                          np.zeros((1024,), np.float32), np.zeros((1024,), np.float32), True)
    ncp0, ncp1 = cp0_.shape[1], cp1_.shape[1]
    with b.es:
        xin = b.dram("x", [TOK, D], F32, "ExternalInput").ap()
        cp0d = b.dram("cp0", [128, ncp0], F32, "ExternalInput").ap()
        cp1d = b.dram("cp1", [128, ncp1], F32, "ExternalInput").ap()
        w0d = b.dram("w0m", [D, 3072], F32, "ExternalInput").ap()
        wald = b.dram("w0a", [D, 16], F32, "ExternalInput").ap()
        wo0d = b.dram("wo0", [D, D], F32, "ExternalInput").ap()
        w1d = b.dram("w1m", [D, 2560], F32, "ExternalInput").ap()
        wo1d = b.dram("wo1", [D, D], F32, "ExternalInput").ap()
        posd = b.dram("pos", [128, NT], I32, "ExternalInput").ap()
        flgd = b.dram("flag", [128, 1], F32, "ExternalInput").ap()
        yout = b.dram("y", [TOK, D], F32, "ExternalOutput").ap()
        x1s = b.dram("x1s", [TOK, D], F32).ap()
        stA_t = b.dram("stA", [64, 512], F32)
        stG_t = b.dram("stG", [128, 512], F32)
        stB_t = b.dram("stB", [192, 512], F32)
        stH_t = b.dram("stH", [384, 512], F32)
        stA, stG, stB, stH = stA_t.ap(), stG_t.ap(), stB_t.ap(), stH_t.ap()

        psum = (b.ps("tp", [128, 8, 128], BF16),) + tuple(b.ps(f"b{i}", [128, 512]) for i in range(1, 8))
        identb = b.tsb("identb", [128, 128], BF16)
        identf = b.tsb("identf", [128, 128])
        flg = b.tsb("flg", [128, 1])
        posi = b.tsb("posi", [128, NT], I32)
        posf = b.tsb("posf", [128, NT])
        wmA1 = b.tsb("wmA1", [128, 8, 1024], BF16)
        P.add("sp", lambda e: e.dma_start(out=flg[:, :], in_=flgd[:, :]), writes=["flg"], dma="c4")
        P.add("sp", lambda e: e.dma_start(out=posi[:, :], in_=posd[:, :]), writes=["posi"], dma="c2")
        P.add("dve", lambda e: e.tensor_copy(posf[:, :], posi[:, :]), reads=["posi"], writes=["posf"])
        P.add("sp", lambda e: e.dma_start(out=identf[:, :], in_=cp0d[:, 0:128]), writes=["identf"], dma="c5")
        P.add("dve", lambda e: e.tensor_copy(identb[:, :], identf[:, :]), reads=["identf"], writes=["identb"])

        NSTG = 4
        stg = [b.tsb(f"stg{i}", [128, 512]) for i in range(NSTG)]
        b.bg = []
        b.stg_i = 0

        def wload(dst, dcol, src, scol, ncols, key, now=False):
            for kc in range(8):
                for c0 in range(0, ncols, 512):
                    n = min(512, ncols - c0)

                    def job(kc=kc, c0=c0, n=n):
                        i = b.stg_i % NSTG
                        b.stg_i += 1
                        P.add("sp", lambda e: e.dma_start(out=stg[i][:, 0:n], in_=src[kc * 128:(kc + 1) * 128, scol + c0:scol + c0 + n]),
                              writes=[f"stg{i}"], dma=f"sg{i}")
                        P.add("act", lambda e: e.copy(dst[:, kc, dcol + c0:dcol + c0 + n], stg[i][:, 0:n]),
                              reads=[f"stg{i}"], writes=[key])
                    if now:
                        job()
                    else:
                        b.bg.append(job)

        def bg_flush():
            while b.bg:
                b.bg.pop(0)()
        b.bg_flush = bg_flush

        with contextlib.ExitStack() as es0:
            cp0 = b.tsb("cp0", [128, ncp0], es=es0)
            wm0 = b.tsb("wm0", [128, 8, 3072], BF16, es=es0)
            wal = b.tsb("wal", [128, 8, 16], BF16, es=es0)
            wo0 = b.tsb("wo0", [128, 8, D], BF16, es=es0)
            P.add("sp", lambda e: e.dma_start(out=cp0[:, :], in_=cp0d[:, :]), writes=["cp"], dma="c0")
            wload(wm0, 0, w0d, 0, 1024, "wm0A", now=True)
            wload(wal, 0, wald, 0, 16, "wal", now=True)
            wload(wm0, 1024, w0d, 1024, 2048, "wm0B")
            wload(wo0, 0, wo0d, 0, 1024, "wo")
            wload(wmA1, 0, w1d, 0, 1024, "wm1A")
            b.cp, b.offs = cp0, offs0

            def wmf0(kc, off, n):
                return wm0[:, kc, off:off + n], ("wm0A" if off < 1024 else "wm0B")
            env = dict(xin=xin, wm=wmf0, identb=identb, wal=wal, wo=wo0, flg=flg, psum=psum, xkey=None,
                       st_local=stA, st_local_t=stA_t, st_gath_t=stG_t, st_gath_key="stG")
            NPH = int(_os.environ.get("K_PH", 4))
            if not _os.environ.get("K_SKIP0S"):
                phase(b, "L0S", env)
                P.barrier()
            env.update(yout=x1s, ykey="x1s", sinit=stG[0:64, :], sinit_key="stG")
            bg_flush()
            if NPH >= 2 and not _os.environ.get("K_SKIP0F"):
                phase(b, "L0F", env)
                P.barrier()

        with contextlib.ExitStack() as es1:
            cp1 = b.tsb("cp1", [128, ncp1], es=es1)
            wmB1 = b.tsb("wmB1", [128, 8, 1536], BF16, es=es1)
            wo1 = b.tsb("wo1", [128, 8, D], BF16, es=es1)
            P.add("sp", lambda e: e.dma_start(out=cp1[:, :], in_=cp1d[:, :]), writes=["cp"], dma="c6")
            wload(wmB1, 0, w1d, 1024, 1536, "wm1B")
            wload(wo1, 0, wo1d, 0, 1024, "wo")
            b.cp, b.offs = cp1, offs1

            def wmf1(kc, off, n):
                if off < 1024:
                    return wmA1[:, kc, off:off + n], "wm1A"
                return wmB1[:, kc, off - 1024:off - 1024 + n], "wm1B"
            env = dict(xin=x1s, wm=wmf1, identb=identb, wo=wo1, flg=flg, psum=psum, xkey="x1s", posf=posf,
                       st_local=stB, st_local_t=stB_t, st_gath_t=stH_t, st_gath_key="stH")
            if NPH >= 3 and not _os.environ.get("K_SKIP1S"):
                phase(b, "L1S", env)
                P.barrier()
            env.update(yout=yout, ykey=None, sinit=stH[0:64, :], pprev=stH[64:192, :], sinit_key="stH")
            bg_flush()
            if NPH >= 4:
                phase(b, "L1F", env)
        if P.ops:
            P.emit(nc, b.es)
    return nc, offs0, offs1


_CACHE = {}


def _get():
    if "nc" not in _CACHE:
        _CACHE["nc"] = build()
    return _CACHE["nc"]


def kernel(x, positions, l0_w_in, l0_w_a2, l0_b_a, l0_gla_norm_g, l0_sgu_ln_g, l0_sgu_ln_b,
           l0_w_s, l0_b_s, l0_w_out, l0_ln_g, l0_ln_b, l1_w_in, l1_ret_norm_g, l1_w_pool,
           l1_pool_scale, l1_w_out, l1_ln_g, l1_ln_b):
    f = lambda a: np.ascontiguousarray(np.asarray(a), dtype=np.float32)
    x = f(x)
    positions = np.ascontiguousarray(np.asarray(positions), dtype=np.int32)
    w0 = f(l0_w_in)
    w0m = np.ascontiguousarray(np.concatenate(
        [w0[:, 0:256], w0[:, 256:512], w0[:, 512:1024], w0[:, 1024:1536], w0[:, 1552:2064], w0[:, 2064:2576], w0[:, 2576:3088]], axis=1))
    w0a = np.ascontiguousarray(w0[:, 1536:1552])
    w1m = f(l1_w_in)
    wo0, wo1 = f(l0_w_out), f(l1_w_out)
    cp0, offs0 = l0_pack(f(l0_w_a2), f(l0_b_a), f(l0_gla_norm_g), f(l0_sgu_ln_g), f(l0_sgu_ln_b), f(l0_w_s), f(l0_b_s),
                         f(l0_ln_g), f(l0_ln_b))
    cp1 = []
    for hf in range(2):
        c_, offs1 = l1_pack(f(l1_ret_norm_g), f(l1_w_pool), f(l1_pool_scale), f(l1_ln_g), f(l1_ln_b), hf == 0)
        cp1.append(c_)
    nc, o0, o1 = _get()
    assert o0 == offs0 and o1 == offs1
    in_maps = []
    for c in range(NCORES):
        bi, hf = c // 2, c % 2
        in_maps.append({
            "x": np.ascontiguousarray(x[bi, hf * TOK:(hf + 1) * TOK, :]),
            "cp0": cp0, "cp1": cp1[hf], "w0m": w0m, "w0a": w0a, "wo0": wo0, "w1m": w1m, "wo1": wo1,
            "pos": np.ascontiguousarray(positions[bi, hf * TOK:(hf + 1) * TOK].reshape(NT, 128).T),
            "flag": np.full((128, 1), float(hf), np.float32),
        })
    res = run_bass_kernel_spmd(nc, in_maps, core_ids=list(range(NCORES)))
    out = np.empty((4, 2 * TOK, D), np.float32)
    for c in range(NCORES):
        out[c // 2, (c % 2) * TOK:(c % 2 + 1) * TOK, :] = np.asarray(res.results[c]["y"], np.float32)
    return out
```

```python
import contextlib
import math
import numpy as np
import ml_dtypes
import concourse.bass as bass
import concourse.mybir as mybir
from concourse.bass_utils import run_bass_kernel_spmd

F32 = mybir.dt.float32
BF16 = mybir.dt.bfloat16
I32 = mybir.dt.int32
ALU = mybir.AluOpType
AF = mybir.ActivationFunctionType
AX = mybir.AxisListType

NCORES = 8
D = 1024
TOK = 4096
NT = TOK // 128
import os as _os
NT_RUN = int(_os.environ.get('K_NT', NT))
STOP = int(_os.environ.get('K_STOP', 99))
DBG = int(_os.environ.get('K_DBG', 0))
ALPHA = float(4 ** 0.25)
EPS = 1e-5
GC0 = math.sqrt(2.0 / math.pi)
GC1 = 0.044715 * GC0


class Op:
    pass


class Prog:
    ENGS = ("pe", "act", "dve", "pool", "sp")

    def __init__(self):
        self.ops = []
        self.cnt = {}
        self.semh = {}

    def add(self, eng, fn, reads=(), writes=(), dma=None, inc=16, xdeps=()):
        op = Op()
        op.eng, op.fn, op.reads, op.writes, op.dma = eng, fn, tuple(reads), tuple(writes), dma
        op.inc = inc
        op.xdeps = list(xdeps)
        op.need = False
        op.deps = []
        self.ops.append(op)
        return op

    def capture(self, fn):
        n = len(self.ops)
        fn()
        out = self.ops[n:]
        del self.ops[n:]
        return out

    @staticmethod
    def merge(a, b):
        if _os.environ.get("K_MERGE") == "pe" and a and b:
            wa = [1.0 if o.eng == "pe" else 0.05 for o in a]
            wb = [1.0 if o.eng == "pe" else 0.05 for o in b]
            ta, tb = sum(wa), sum(wb)
            out, i, j, ca, cb = [], 0, 0, 0.0, 0.0
            while i < len(a) or j < len(b):
                if j >= len(b) or (i < len(a) and ca / ta <= cb / tb):
                    out.append(a[i]); ca += wa[i]; i += 1
                else:
                    out.append(b[j]); cb += wb[j]; j += 1
            return out
        out, i, j = [], 0, 0
        while i < len(a) or j < len(b):
            if j >= len(b) or (i < len(a) and i * len(b) <= j * len(a)):
                out.append(a[i]); i += 1
            else:
                out.append(b[j]); j += 1
        return out

    def barrier(self):
        return
        last = {}
        for op in self.ops:
            if op.fn is None:
                continue
            if op.dma is not None and not (op.dma.startswith("os") or op.dma.startswith("of") or op.dma.startswith("cc")):
                continue
            last[op.dma if op.dma is not None else op.eng] = op
        for e in self.ENGS:
            self.add(e, None, xdeps=list(last.values()))

    def plan(self):
        last_w, readers = {}, {}
        for op in self.ops:
            deps = []
            for k in op.reads:
                w = last_w.get(k)
                if w is not None:
                    deps.append((w, "RAW"))
            for k in op.writes:
                w = last_w.get(k)
                if w is not None:
                    deps.append((w, "WAW"))
                for r in readers.get(k, ()):
                    deps.append((r, "WAR"))
            for p in op.xdeps:
                if p not in op.deps:
                    op.deps.append(p)
                    p.need = True
            for p, kind in deps:
                if p is op:
                    continue
                if p.dma is None and op.dma is None and p.eng == op.eng:
                    if op.eng == "pe" or kind != "RAW":
                        continue
                if p not in op.deps:
                    op.deps.append(p)
                    p.need = True
            for k in op.reads:
                readers.setdefault(k, []).append(op)
            for k in op.writes:
                last_w[k] = op
                readers[k] = []
        cnt = self.cnt
        for op in self.ops:
            if op.dma is not None:
                cnt[op.dma] = cnt.get(op.dma, 0) + op.inc
                op.sem, op.val = op.dma, cnt[op.dma]
            elif op.need:
                cnt[op.eng] = cnt.get(op.eng, 0) + 1
                op.sem, op.val = op.eng, cnt[op.eng]
        self.sem_names = sorted(set(cnt.keys()) | set(self.ENGS))

    def emit(self, nc, es):
        self.plan()
        for n in self.sem_names:
            if n not in self.semh:
                self.semh[n] = es.enter_context(nc.semaphore("s_" + n))
        semh = self.semh
        by_eng = {e: [op for op in self.ops if op.eng == e] for e in self.ENGS}

        def run(eh, eng):
            waited = {}
            for op in by_eng[eng]:
                wl = {}
                for p in op.deps:
                    wl[p.sem] = max(wl.get(p.sem, 0), p.val)
                for s, v in wl.items():
                    if waited.get(s, 0) >= v:
                        continue
                    eh.wait_ge(semh[s], v)
                    waited[s] = v
                if op.fn is None:
                    continue
                ins = op.fn(eh)
                if op.dma is not None:
                    ins.then_inc(semh[op.dma], op.inc)
                elif op.need:
                    ins.then_inc(semh[eng], 1)

        with nc.Block() as block:
            @block.tensor
            def _(e):
                run(e, "pe")

            @block.scalar
            def _(e):
                run(e, "act")

            @block.vector
            def _(e):
                run(e, "dve")

            @block.gpsimd
            def _(e):
                run(e, "pool")

            @block.sync
            def _(e):
                run(e, "sp")
        self.ops = []


def _pack(items):
    offs, cols, o = {}, [], 0
    for name, arr in items:
        arr = np.ascontiguousarray(arr, dtype=np.float32).reshape(128, -1)
        offs[name] = (o, arr.shape[1])
        cols.append(arr)
        o += arr.shape[1]
    return np.concatenate(cols, axis=1), offs


def _rep(v):
    v = np.asarray(v, np.float32).reshape(1, -1)
    return np.broadcast_to(v, (128, v.shape[1]))


def l0_pack(w_a2, b_a, gla_g, sgu_g, sgu_b, w_s, b_s, ln_g, ln_b):
    j = np.arange(128)[:, None]
    i = np.arange(128)[None, :]
    same = (j // 64) == (i // 64)
    mincl = np.where(same & (j <= i), -1.0 / 16.0, 0.0)
    mafter = np.where(same & (j > i), -1.0 / 16.0, 0.0)
    cm = np.where(same & (j <= i), 1.0, 0.0)
    wa2 = np.zeros((128, 256), np.float32)
    wa2[0:16] = w_a2
    wa2[16] = b_a
    items = [
        ("ident", np.eye(128)),
        ("mincl", mincl), ("mafter", mafter),
        ("negs", np.stack([np.where(np.arange(128) < 64, -1.0 / 16.0, 0.0), np.where(np.arange(128) >= 64, -1.0 / 16.0, 0.0)], axis=1)),
        ("cmask", np.tile(cm, (1, 4))),
        ("gainA", _rep(gla_g.reshape(-1))),
        ("sguG", _rep(sgu_g.reshape(-1))), ("sguB", _rep(sgu_b.reshape(-1))),
        ("wsT", np.transpose(w_s, (2, 0, 1)).reshape(128, 512)),
        ("trilT", np.tile(np.where(j <= i, 1.0, 0.0), (1, 4))),
        ("bs", np.transpose(b_s, (1, 0))),
        ("lng", _rep(ln_g)), ("lnb", _rep(ln_b)),
        ("wa2", wa2),
        ("neghalf", np.full((128, 4), -0.5)),
        ("ones", np.ones((128, 128))),
        ("mafterF", np.where(j > i, -1.0 / 16.0, 0.0)),
        ("negsF", np.full((128, 1), -1.0 / 16.0)),
    ]
    return _pack(items)


def l1_pack(ret_g, w_pool, pool_scale, ln_g, ln_b, first_half):
    H, C = 4, 128
    lg = np.log(1.0 - 2.0 ** (-5.0 - np.arange(H, dtype=np.float64)))
    idx = np.arange(C, dtype=np.float64)
    half = 32
    inv = 10000.0 ** (-np.arange(half, dtype=np.float64) / half) / (2.0 * np.pi)
    invR = np.tile(inv[None, :], (8, 1)).reshape(-1)
    jj = idx[:, None]
    ii = idx[None, :]
    dm = np.zeros((128, H, 128))
    for h in range(H):
        dm[:, h, :] = np.where(ii >= jj, np.exp(np.maximum(ii - jj, 0.0) * lg[h]), 0.0) / 8.0
    winter = np.exp((idx[:, None] + 1.0) * lg[None, :]) / 8.0
    wstate = np.exp((C - 1.0 - idx)[:, None] * lg[None, :])
    winterR = np.repeat(winter[:, :, None], 64, axis=2).reshape(128, 256)
    wstateR = np.repeat(wstate[:, :, None], 64, axis=2).reshape(128, 256)
    gd = np.exp(C * lg)
    gdec = np.broadcast_to(gd[None, :], (128, 4))
    wins = (2, 4, 8, 16)
    s = np.arange(128)[:, None]
    t = np.arange(128)[None, :]
    gcur = np.zeros((128, 4, 128)); gprev = np.zeros((128, 4, 128)); icn = np.zeros((128, 4, 128))
    gcur0 = np.zeros((128, 4, 128)); gprev0 = np.zeros((128, 4, 128)); icn0 = np.zeros((128, 4, 128))
    for g, w in enumerate(wins):
        ind = ((s <= t) & (s > t - w)).astype(np.float64)
        gcur[:, g, :] = ind - w * (s == t)
        gprev[:, g, :] = ((s - 128) > (t - w)).astype(np.float64)
        icn[:, g, :] = 1.0 / w
        if first_half:
            cntt = np.minimum(t + 1, w).astype(np.float64)
            gcur0[:, g, :] = ind - cntt * (s == t)
            gprev0[:, g, :] = 0.0
            icn0[:, g, :] = np.broadcast_to(1.0 / cntt, (128, 128))
        else:
            gcur0[:, g, :] = gcur[:, g, :]; gprev0[:, g, :] = gprev[:, g, :]; icn0[:, g, :] = icn[:, g, :]
    items = [
        ("ident", np.eye(128)),
        ("invR", _rep(invR)),
        ("dmatT", dm.reshape(128, 512)),
        ("winterR", winterR), ("wstateR", wstateR),
        ("gdec", gdec),
        ("gainC", _rep(ret_g.reshape(-1))),
        ("gcur", gcur.reshape(128, 512)), ("gprev", gprev.reshape(128, 512)), ("icn", icn.reshape(128, 512)),
        ("gcur0", gcur0.reshape(128, 512)), ("gprev0", gprev0.reshape(128, 512)), ("icn0", icn0.reshape(128, 512)),
        ("wpool", np.transpose(w_pool, (1, 0, 2)).reshape(128, 512)),
        ("pscale", _rep(pool_scale)),
        ("lng", _rep(ln_g)), ("lnb", _rep(ln_b)),
        ("neghalf", np.full((128, 4), -0.5)),
    ]
    return _pack(items)


class B:
    def __init__(self):
        self.nc = bass.Bass("TRN2", target_bir_lowering=False)
        self.P = Prog()
        self.es = contextlib.ExitStack()
        self.pes = None
        self.pfx = ""
        self.offs = None
        self.cp = None

    def sb(self, name, shape, dt=F32):
        return self.pes.enter_context(self.nc.sbuf_tensor("sb_" + self.pfx + name, list(shape), dt))

    def tsb(self, name, shape, dt=F32, es=None):
        return (es or self.es).enter_context(self.nc.sbuf_tensor("sb_" + name, list(shape), dt))

    def ps(self, name, shape, dt=F32):
        return self.es.enter_context(self.nc.psum_tensor("ps_" + name, list(shape), dt))

    def dram(self, name, shape, dt, kind="Internal"):
        return self.nc.dram_tensor(name, list(shape), dt, kind=kind)

    def c(self, name, lo=0, hi=None, rows=slice(None)):
        o, n = self.offs[name]
        hi = n if hi is None else hi
        return self.cp[rows, o + lo:o + hi]


def phase(b, mode, env):
    nc, P = b.nc, b.P
    L = 0 if mode.startswith("L0") else 1
    full = mode.endswith("F")
    b.pfx = mode + "_"
    STOPL = int(_os.environ.get(f"K_STOP{L}", STOP))
    cp_, offs_ = b.cp, b.offs

    def c_(name, lo=0, hi=None, rows=slice(None)):
        o, n = offs_[name]
        hi = n if hi is None else hi
        return cp_[rows, o + lo:o + hi]
    xin, yout = env["xin"], env.get("yout")
    wmf, identb = env["wm"], env["identb"]
    wal, wo = env.get("wal"), env.get("wo")
    posf = env.get("posf")
    flg = env["flg"]
    tp, b1, b2, b3, b4, b5, b6, b7 = env["psum"]
    xkey = env["xkey"]
    ykey = env.get("ykey")
    with contextlib.ExitStack() as pes:
        b.pes = pes
        x32 = [b.sb(f"x32_{i}", [128, D]) for i in range(2)]
        xb = [b.sb(f"xb_{i}", [128, D], BF16) for i in range(2)]
        xT = [b.sb(f"xT_{i}", [128, 8, 128], BF16) for i in range(2)]
        vbs = [b.sb(f"vb_{i}", [128, 512], BF16) for i in range(2)]
        S32 = b.sb("S32", [64, 4, 128])
        Sb = [b.sb(f"Sb_{i}", [64, 4, 128], BF16) for i in range(2)]
        kst = b.sb("kst", [128, 256], BF16)
        kstc = [b.sb(f"kstc_{i}", [128, 256], BF16) for i in range(2)]
        if L == 0:
            alT = b.sb("alT", [32, 128])
            e1 = b.sb("e1", [128, 256])
            spl = b.sb("spl", [128, 256])
            Eq = b.sb("Eq", [128, 256]); Ek = b.sb("Ek", [128, 256]); Es = b.sb("Es", [128, 256])
            dec = b.sb("dec", [64, 8])
        else:
            ang = b.sb("ang", [128, 512])
            angi = b.sb("angi", [128, 512], I32)
            angf = b.sb("angf", [128, 512])
            gt1 = b.sb("gt1", [128, 512])
            sc_ = b.sb("sincos", [128, 512])
            rt = [b.sb(f"rt{i}", [128, 256]) for i in range(4)]
            qkr = b.sb("qkr", [128, 512])
            plst = b.sb("plst", [128, 512])
        if full:
            qkd = b.sb("qkd", [128, 768], BF16)
            qkT = b.sb("qkT", [64, 12, 128], BF16)
            qm = [b.sb(f"qm_{i}", [64, 4, 128], BF16) for i in range(2)]
            scT = b.sb("scT", [128, 512], BF16)
            sq = b.sb("sq", [128, 512])
            ss = b.sb("ss", [128, 4]); vv = b.sb("vv", [128, 4]); rstd = b.sb("rstd", [128, 4])
            oa1 = b.sb("oa1", [128, 512]); oa2 = b.sb("oa2", [128, 512])
            sg1 = b.sb("sg1", [128, 512]); sg2 = b.sb("sg2", [128, 512])
            cat = b.sb("cat", [128, D], BF16)
            catT = b.sb("catT", [128, 8, 128], BF16)
            bst = b.sb("bst", [128, 4, 6]); bmv = b.sb("bmv", [128, 4, 2])
            bstS = b.sb("bstS", [128, 4, 6]); bmvS = b.sb("bmvS", [128, 4, 2])
            vvS = b.sb("vvS", [128, 4]); rstdS = b.sb("rstdS", [128, 4])
            r32 = b.sb("r32", [128, D])
            bst2 = b.sb("bst2", [128, 12]); bmv2 = b.sb("bmv2", [128, 2])
            vv2 = b.sb("vv2", [128, 1]); rstd2 = b.sb("rstd2", [128, 1])
            n32 = b.sb("n32", [128, D])
            o32 = [b.sb(f"o32_{i}", [128, D]) for i in range(2)]
            if L == 0:
                wsTb = b.sb("wsTb", [128, 4, 128], BF16)
                wsTm = b.sb("wsTm", [128, 512])
                squ = b.sb("squ", [128, 512]); inn = b.sb("inn", [128, 512])
                gu = b.sb("gu", [128, 512]); gsv = b.sb("gsv", [128, 512])
                nrm = b.sb("nrm", [128, 512]); nrm2 = b.sb("nrm2", [128, 512])
                svn = b.sb("svn", [128, 512], BF16)
                t1 = b.sb("t1", [128, 512])
            else:
                pbuf = [b.sb(f"pb_{i}", [128, 512], BF16) for i in range(2)]
                pinit = b.sb("pinit", [128, 512])
                gcb = [b.sb(f"gcb_{i}", [128, 4, 128], BF16) for i in range(2)]
                gpb = [b.sb(f"gpb_{i}", [128, 4, 128], BF16) for i in range(2)]
                wpb = b.sb("wpb", [128, 4, 128], BF16)
                plT = b.sb("plT", [128, 4, 128], BF16)
                on = b.sb("on", [128, 512])

        KX = _os.environ.get("K_X", "")
        if full and not ("1" in KX and L == 0):
            sini = b.sb("sini", [64, 4, 128])
            P.add("sp", lambda e: e.dma_start(out=sini[:, :, :], in_=env["sinit"].rearrange("p (a b) -> p a b", a=4)),
                  reads=[env["sinit_key"]], writes=["sini"], dma="c1" + mode)
            P.add("dve", lambda e: e.tensor_scalar(S32[:, :, :], sini[:, :, :], flg[0:64, 0:1], None, ALU.mult),
                  reads=["sini", "flg"], writes=["S32"])
        else:
            P.add("dve", lambda e: e.memset(S32[:, :, :], 0.0), writes=["S32"])
        P.add("act", lambda e: e.copy(Sb[0][:, :, :], S32[:, :, :]), reads=["S32"], writes=["Sb0"])
        if L == 0:
            P.add("dve", lambda e: e.memset(alT[:, :], 1.0), writes=["alT"])
            for i in range(2):
                if "2" in KX:
                    continue
                P.add("dve", lambda e, i=i: e.memset(kstc[i][:, :], 0.0), writes=[f"kstc{i}"])
                if full:
                    P.add("dve", lambda e, i=i: e.memset(qm[i][:, :, :], 0.0), writes=[f"qm{i}"])
            if full and "3" not in KX:
                P.add("dve", lambda e: e.tensor_tensor(wsTm[:, :], c_("wsT"), c_("trilT"), ALU.mult),
                      reads=["cp"], writes=["wsTm"])
                P.add("dve", lambda e: e.tensor_copy(wsTb[:, :, :], wsTm[:, :].rearrange("p (g t) -> p g t", g=4)),
                      reads=["wsTm"], writes=["wsTb"])
        else:
            if full:
                P.add("sp", lambda e: e.dma_start(out=pinit[:, :], in_=env["pprev"]), reads=[env["sinit_key"]], writes=["pinit"], dma="c3")
                P.add("dve", lambda e: e.tensor_scalar(pbuf[1][:, :], pinit[:, :], flg[:, 0:1], None, ALU.mult),
                      reads=["pinit", "flg"], writes=["pb1"])
                for i, (gc_, gp_) in enumerate((("gcur0", "gprev0"), ("gcur", "gprev"))):
                    P.add("dve", lambda e, i=i, gc_=gc_: e.tensor_copy(
                        gcb[i][:, :, :], c_(gc_).rearrange("p (g t) -> p g t", g=4)), reads=["cp"], writes=[f"gcb{i}"])
                    P.add("dve", lambda e, i=i, gp_=gp_: e.tensor_copy(
                        gpb[i][:, :, :], c_(gp_).rearrange("p (g t) -> p g t", g=4)), reads=["cp"], writes=[f"gpb{i}"])
                P.add("dve", lambda e: e.tensor_copy(wpb[:, :, :], c_("wpool").rearrange("p (g t) -> p g t", g=4)),
                      reads=["cp"], writes=["wpb"])

        def proj(sl, off, n, out_ap, key, extra_reads=()):
            for kc in range(8):
                wap, wkey = wmf(kc, off, n)
                P.add("pe", lambda e, kc=kc, wap=wap: e.matmul(out_ap, xT[sl][:, kc, :], wap,
                                                               start=(kc == 0), stop=(kc == 7)),
                      reads=[f"xT{'a' if kc < 4 else 'b'}{sl}", wkey] + list(extra_reads), writes=[key])

        def load_x(t):
            sl = t % 2
            P.add("sp", lambda e: e.dma_start(out=x32[sl][:, :], in_=xin[t * 128:(t + 1) * 128, :]),
                  reads=([f"{xkey}{t}"] if xkey else []), writes=[f"x32_{sl}"], dma=f"xs{sl}")

        def tile(t, part="all"):
            sl = t % 2
            doH, doG, doS, doT = (part in ("all", x) for x in "HGST")
            if part == "all":
                tpx, tpk = tp, "tp"
            elif full:
                tpx, tpk = b3[:, :].bitcast(BF16).rearrange("p (a c) -> p a c", a=8), "b3"
            else:
                tpx, tpk = b5[:, :].bitcast(BF16).rearrange("p (a c) -> p a c", a=8), "b5"
            if STOPL <= 0:
                return
            early_g1 = full and part != "all" and not _os.environ.get("K_LATEG1")
            if doH:
              head_part(t, sl, tpx, tpk)
              if early_g1:
                g1_part(t, sl)
            if doG:
              if not (early_g1 and part == "G"):
                g1_part(t, sl)
              g2_part(t, sl)
            if full and doS:
              s_part(t, sl)
            if full and doT:
              tail_part(t, sl)

        def pbanks(t):
            if full or t % 2 == 0 or _os.environ.get("K_NOTHREAD"):
                return b1, b2, "b1", "b2"
            return b6, b7, "b6", "b7"

        def head_part(t, sl, tpx, tpk):
            for _ in range(3):
                if b.bg:
                    b.bg.pop(0)()
            if tpk == "tp" and t + 1 < NT_RUN:
                load_x(t + 1)
            P.add("dve", lambda e, sl=sl: e.tensor_copy(xb[sl][:, :], x32[sl][:, :]),
                  reads=[f"x32_{sl}"], writes=[f"xb{sl}"])
            for kc in range(8):
                P.add("pe", lambda e, kc=kc, sl=sl: e.transpose(tpx[:, kc, :], xb[sl][:, kc * 128:(kc + 1) * 128], identb[:, :]),
                      reads=[f"xb{sl}", "identb"], writes=[tpk])
            P.add("act", lambda e, sl=sl: e.copy(xT[sl][:, 0:4, :], tpx[:, 0:4, :]), reads=[tpk], writes=[f"xTa{sl}"])
            P.add("act", lambda e, sl=sl: e.copy(xT[sl][:, 4:8, :], tpx[:, 4:8, :]), reads=[tpk], writes=[f"xTb{sl}"])

            if STOPL <= 1:
                return
            bq, bv, bqk, bvk = pbanks(t)
            if full:
                proj(sl, 0, 512, bq[:, :], bqk)
            else:
                proj(sl, 256, 256, bq[:, 256:512], bqk)
            proj(sl, 512, 512, bv[:, :], bvk)
            P.add("act", lambda e: e.copy(vbs[sl][:, :], bv[:, :]), reads=[bvk], writes=[f"vb{sl}"])
            if full:
                proj(sl, 1024, 512, b6[:, :], "b6")
                proj(sl, 1536, 512, b7[:, :], "b7")
                if L == 0:
                    P.add("act", lambda e: e.activation(sg1[:, :], b6[:, :], AF.Sigmoid), reads=["b6"], writes=["sg1"])
                    P.add("dve", lambda e: e.tensor_tensor(sg2[:, :], b6[:, :], sg1[:, :], ALU.mult), reads=["b6", "sg1"], writes=["sg2"])
                    P.add("pool", lambda e: e.tensor_tensor(oa2[:, :], sg2[:, :], c_("gainA"), ALU.mult), reads=["sg2", "cp"], writes=["gs"])
                else:
                    P.add("act", lambda e: e.activation(sg2[:, :], b6[:, :], AF.Silu), reads=["b6"], writes=["sg2"])
                    P.add("pool", lambda e: e.tensor_tensor(oa2[:, :], sg2[:, :], c_("gainC"), ALU.mult), reads=["sg2", "cp"], writes=["gs"])

        def g1_part(t, sl):
            bq, bv, bqk, bvk = pbanks(t)
            if STOPL <= 2:
                return
            if L == 0:
                for kc in range(8):
                    P.add("pe", lambda e, kc=kc, sl=sl: e.matmul(b3[0:16, 256:384], wal[:, kc, :], xT[sl][:, kc, :],
                                                                 start=(kc == 0), stop=(kc == 7)),
                          reads=[f"xT{'a' if kc < 4 else 'b'}{sl}", "wal"], writes=["b3"])
                P.add("dve", lambda e: e.tensor_copy(alT[0:16, :], b3[0:16, 256:384]), reads=["b3"], writes=["alT"])
                P.add("pe", lambda e: e.matmul(b3[:, 0:256], alT[0:17, :], c_("wa2", rows=slice(0, 17)), start=True, stop=True),
                      reads=["alT", "cp"], writes=["b3"])
                P.add("act", lambda e: e.activation(e1[:, :], b3[:, 0:256], AF.Exp, scale=-1.0), reads=["b3"], writes=["e1"])
                P.add("act", lambda e: e.activation(spl[:, :], e1[:, :], AF.Ln, bias=1.0), reads=["e1"], writes=["spl"])
                TS = (not full) and not _os.environ.get("K_NOTS")
                if TS:
                    P.add("pe", lambda e: e.matmul(b4[:, 256:512], c_("mafterF"), spl[:, :], start=True, stop=True),
                          reads=["spl", "cp"], writes=["b4"])
                    for h in range(4):
                        P.add("pe", lambda e, h=h: e.matmul(
                            b3[0:64, 384 + h * 2:386 + h * 2], spl[:, h * 64:(h + 1) * 64],
                            c_("negs", 0, 2), start=True, stop=True),
                            reads=["spl", "cp"], writes=["b3"])
                else:
                    P.add("pe", lambda e: e.matmul(b4[:, 0:256], c_("mincl"), spl[:, :], start=True, stop=True),
                          reads=["spl", "cp"], writes=["b4"])
                    P.add("pe", lambda e: e.matmul(b4[:, 256:512], c_("mafter"), spl[:, :], start=True, stop=True),
                          reads=["spl", "cp"], writes=["b4"])
                    for h in range(4):
                        P.add("pe", lambda e, h=h: e.matmul(
                            b3[0:64, 384 + h * 2:386 + h * 2], spl[:, h * 64:(h + 1) * 64],
                            c_("negs", 0, 2), start=True, stop=True),
                            reads=["spl", "cp"], writes=["b3"])
                if full:
                    P.add("act", lambda e: e.activation(Eq[:, :], b4[:, 0:256], AF.Exp), reads=["b4"], writes=["Eq"])
                    P.add("act", lambda e: e.activation(Ek[:, :], b4[:, 0:256], AF.Exp, scale=-1.0), reads=["b4"], writes=["Ek"])
                P.add("act", lambda e: e.activation(Es[:, :], b4[:, 256:512], AF.Exp), reads=["b4"], writes=["Es"])
                P.add("act", lambda e: e.activation(dec[:, :], b3[0:64, 384:392], AF.Exp), reads=["b3"], writes=["dec"])
                if full:
                    P.add("dve", lambda e: e.scalar_tensor_tensor(qkd[:, 0:256], b1[:, 0:256], 0.125, Eq[:, :], ALU.mult, ALU.mult),
                          reads=["b1", "Eq"], writes=["qkd"])
                    P.add("dve", lambda e: e.tensor_tensor(qkd[:, 256:512], b1[:, 256:512], Ek[:, :], ALU.mult),
                          reads=["b1", "Ek"], writes=["qkd"])
                if TS:
                    P.add("dve", lambda e: e.tensor_tensor(kst[:, :], bq[:, 256:512], Es[:, :], ALU.mult),
                          reads=[bqk, "Es"], writes=["kst"])
                    decv = dec[:, :].rearrange("p (h c) -> p h c", c=2)
                    P.add("dve", lambda e: e.tensor_tensor(decv[:, :, 0], decv[:, :, 0], decv[:, :, 1], ALU.mult),
                          reads=["dec"], writes=["dec"])
                else:
                    for c in range(2):
                        P.add("dve", lambda e, c=c: e.tensor_tensor(kstc[c][64 * c:64 * c + 64, :], bq[64 * c:64 * c + 64, 256:512],
                                                                    Es[64 * c:64 * c + 64, :], ALU.mult),
                              reads=[bqk, "Es"], writes=[f"kstc{c}"])
                nblk = 8
            else:
                if t % 8 == 0:
                    n8 = min(8, NT_RUN - t)
                    w8 = n8 * 64
                    angv = ang[:, 0:w8].rearrange("p (j s f) -> p j s f", s=2, f=32)
                    P.add("dve", lambda e: e.tensor_tensor(
                        angv[:, :, 0, :], posf[:, t:t + n8].unsqueeze(2).broadcast_to([128, n8, 32]),
                        c_("invR", 0, 32).unsqueeze(1).broadcast_to([128, n8, 32]), ALU.mult),
                        reads=["cp", "posf"], writes=["ang"])
                    P.add("dve", lambda e: e.tensor_scalar(angv[:, :, 1, :], angv[:, :, 0, :], 0.25, None, ALU.add),
                          reads=["ang"], writes=["ang"])
                    P.add("dve", lambda e: e.tensor_copy(angi[:, 0:w8], ang[:, 0:w8]), reads=["ang"], writes=["angi"])
                    P.add("dve", lambda e: e.tensor_copy(angf[:, 0:w8], angi[:, 0:w8]), reads=["angi"], writes=["angf"])
                    P.add("dve", lambda e: e.tensor_tensor(ang[:, 0:w8], ang[:, 0:w8], angf[:, 0:w8], ALU.subtract),
                          reads=["ang", "angf"], writes=["ang"])
                    P.add("dve", lambda e: e.tensor_scalar(gt1[:, 0:w8], ang[:, 0:w8], 0.5, None, ALU.is_gt), reads=["ang"], writes=["gt1"])
                    P.add("dve", lambda e: e.tensor_tensor(ang[:, 0:w8], ang[:, 0:w8], gt1[:, 0:w8], ALU.subtract),
                          reads=["ang", "gt1"], writes=["ang"])
                    P.add("dve", lambda e: e.tensor_scalar(gt1[:, 0:w8], ang[:, 0:w8], -0.5, None, ALU.is_lt), reads=["ang"], writes=["gt1"])
                    P.add("dve", lambda e: e.tensor_tensor(ang[:, 0:w8], ang[:, 0:w8], gt1[:, 0:w8], ALU.add),
                          reads=["ang", "gt1"], writes=["ang"])
                    P.add("act", lambda e: e.activation(sc_[:, 0:w8], ang[:, 0:w8], AF.Sin, scale=2.0 * math.pi), reads=["ang"], writes=["sincos"])
                lo = 0 if full else 4
                nh = 8 - lo

                def v4(ap):
                    return ap.rearrange("p (a two c) -> p a two c", two=2, c=32)
                hq = v4(bq[:, :])
                j8 = (t % 8) * 64
                sinv = sc_[:, j8:j8 + 32].unsqueeze(1).broadcast_to([128, 8, 32])
                cosv = sc_[:, j8 + 32:j8 + 64].unsqueeze(1).broadcast_to([128, 8, 32])
                rtv = [r[:, :].rearrange("p (a c) -> p a c", c=32) for r in rt]
                qv = v4(qkr[:, :])
                for i_, (half_, tab) in enumerate(((0, cosv), (1, sinv), (0, sinv), (1, cosv))):
                    P.add("dve", lambda e, i_=i_, half_=half_, tab=tab: e.tensor_tensor(
                        rtv[i_][:, lo:8, :], hq[:, lo:8, half_, :], tab[:, lo:8, :], ALU.mult),
                        reads=[bqk, "sincos"], writes=[f"rt{i_}"])
                P.add("dve", lambda e: e.tensor_tensor(qv[:, lo:8, 0, :], rtv[0][:, lo:8, :], rtv[1][:, lo:8, :], ALU.subtract),
                      reads=["rt0", "rt1"], writes=["qkr"])
                P.add("dve", lambda e: e.tensor_tensor(qv[:, lo:8, 1, :], rtv[2][:, lo:8, :], rtv[3][:, lo:8, :], ALU.add),
                      reads=["rt2", "rt3"], writes=["qkr"])
                P.add("pool", lambda e: e.tensor_tensor(kst[:, :], qkr[:, 256:512], c_("wstateR"), ALU.mult),
                      reads=["qkr", "cp"], writes=["kst"])
                if full:
                    P.add("act", lambda e: e.copy(qkd[:, 0:512], qkr[:, :]), reads=["qkr"], writes=["qkd"])
                    P.add("dve", lambda e: e.tensor_tensor(qkd[:, 512:768], qkr[:, 0:256], c_("winterR"), ALU.mult),
                          reads=["qkr", "cp"], writes=["qkd"])
                nblk = 12

        def g2_part(t, sl):
            bq, bv, bqk, bvk = pbanks(t)
            nblk = 8 if L == 0 else 12
            if STOPL <= 3:
                return
            def l0_U(c):
                for h in range(4):
                    P.add("pe", lambda e, h=h, c=c: e.matmul(
                        b4[0:64, h * 128:(h + 1) * 128], kstc[c][:, h * 64:(h + 1) * 64], vbs[sl][:, h * 128:(h + 1) * 128],
                        start=True, stop=True), reads=[f"kstc{c}", f"vb{sl}"], writes=["b4"])

            def l0_update(c):
                for h in range(4):
                    P.add("dve", lambda e, c=c, h=h: e.scalar_tensor_tensor(
                        S32[:, h, :], S32[:, h, :], dec[:, h * 2 + c:h * 2 + c + 1],
                        b4[0:64, h * 128:(h + 1) * 128], ALU.mult, ALU.add),
                        reads=["S32", "dec", "b4"], writes=["S32"])
                nxt = 1 - c
                if full:
                    P.add("act", lambda e, nxt=nxt: e.copy(Sb[nxt][:, :, :], S32[:, :, :]), reads=["S32"], writes=[f"Sb{nxt}"])

            TS = (not full) and L == 0 and not _os.environ.get("K_NOTS")
            if TS:
                for h in range(4):
                    P.add("pe", lambda e, h=h: e.matmul(
                        b4[0:64, h * 128:(h + 1) * 128], kst[:, h * 64:(h + 1) * 64], vbs[sl][:, h * 128:(h + 1) * 128],
                        start=True, stop=True), reads=["kst", f"vb{sl}"], writes=["b4"])
                for h in range(4):
                    P.add("dve", lambda e, h=h: e.scalar_tensor_tensor(
                        S32[:, h, :], S32[:, h, :], dec[:, h * 2:h * 2 + 1],
                        b4[0:64, h * 128:(h + 1) * 128], ALU.mult, ALU.add),
                        reads=["S32", "dec", "b4"], writes=["S32"])
                return
            if L == 0:
                l0_U(0)
                if STOPL <= 4:
                    return
                l0_update(0)
                if STOPL <= 5:
                    return
            if full:
                for blk in range(8):
                    P.add("pe", lambda e, blk=blk: e.transpose(tp[0:64, blk, :], qkd[:, blk * 64:(blk + 1) * 64], identb[:, :]),
                          reads=["qkd", "identb"], writes=["tp"])
                P.add("act", lambda e: e.copy(qkT[:, 0:8, :], tp[0:64, 0:8, :]), reads=["tp"], writes=["qkT"])
                if L == 0:
                    for c in range(2):
                        P.add("act", lambda e, c=c: e.copy(qm[c][:, :, 64 * c:64 * c + 64], tp[0:64, 0:4, 64 * c:64 * c + 64]),
                              reads=["tp"], writes=[f"qm{c}"])
                for h in range(4):
                    P.add("pe", lambda e, h=h: e.matmul(
                        b5[:, h * 128:(h + 1) * 128], qkT[:, 4 + h, :], qkT[:, h, :],
                        start=True, stop=True), reads=["qkT"], writes=["b5"])
                if L == 1:
                    for blk in range(4):
                        P.add("pe", lambda e, blk=blk: e.transpose(tp[0:64, blk, :], qkd[:, 512 + blk * 64:512 + (blk + 1) * 64], identb[:, :]),
                              reads=["qkd", "identb"], writes=["tp"])
                    P.add("act", lambda e: e.copy(qkT[:, 8:12, :], tp[0:64, 0:4, :]), reads=["tp"], writes=["qkT2"])
                mk = "cmask" if L == 0 else "dmatT"
                P.add("dve", lambda e: e.tensor_tensor(scT[:, :], b5[:, :], c_(mk), ALU.mult), reads=["b5", "cp"], writes=["scT"])
                for h in range(4):
                    P.add("pe", lambda e, h=h: e.matmul(b1[:, h * 128:(h + 1) * 128], scT[:, h * 128:(h + 1) * 128],
                                                        vbs[sl][:, h * 128:(h + 1) * 128], start=True, stop=False),
                          reads=["scT", f"vb{sl}"], writes=["b1"])
                    if L == 0:
                        for c in range(2):
                            P.add("pe", lambda e, h=h, c=c: e.matmul(
                                b1[:, h * 128:(h + 1) * 128], qm[c][:, h, :], Sb[c][:, h, :], start=False, stop=(c == 1)),
                                reads=[f"qm{c}", f"Sb{c}"], writes=["b1"])
                    else:
                        P.add("pe", lambda e, h=h: e.matmul(
                            b1[:, h * 128:(h + 1) * 128], qkT[:, 8 + h, :], Sb[0][:, h, :],
                            start=False, stop=True), reads=["qkT2", "Sb0"], writes=["b1"])

            if L == 0:
                l0_U(1)
                l0_update(1)
            else:
                for h in range(4):
                    P.add("pe", lambda e, h=h: e.matmul(
                        b4[0:64, h * 128:(h + 1) * 128], kst[:, h * 64:(h + 1) * 64],
                        vbs[sl][:, h * 128:(h + 1) * 128], start=True, stop=True), reads=["kst", f"vb{sl}"], writes=["b4"])
                for h in range(4):
                    P.add("dve", lambda e, h=h: e.scalar_tensor_tensor(
                        S32[:, h, :], S32[:, h, :], c_("gdec", h, h + 1, rows=slice(0, 64)), b4[0:64, h * 128:(h + 1) * 128], ALU.mult, ALU.add),
                        reads=["S32", "cp", "b4"], writes=["S32"])
                if full:
                    P.add("act", lambda e: e.copy(Sb[0][:, :, :], S32[:, :, :]), reads=["S32"], writes=["Sb0"])
                if not full and t == NT_RUN - 1:
                    proj(sl, 1536, 512, b6[:, :], "b6")
                    P.add("act", lambda e: e.copy(plst[:, :], b6[:, :]), reads=["b6"], writes=["plst"])

            if not full:
                return

            if L == 0:
                P.add("act", lambda e: e.activation(sq[:, :], b1[:, :], AF.Square), reads=["b1"], writes=["sq"])
                P.add("dve", lambda e: e.tensor_reduce(ss[:, :], sq[:, :].rearrange("p (h c) -> p h c", h=4), AX.X, ALU.add),
                      reads=["sq"], writes=["ss"])
                P.add("pool", lambda e: e.tensor_scalar(vv[:, :], ss[:, :], 1.0 / 128.0, EPS, ALU.mult, ALU.add), reads=["ss"], writes=["vv"])
                P.add("pool", lambda e: e.tensor_tensor(rstd[:, :], vv[:, :], c_("neghalf"), ALU.pow), reads=["vv", "cp"], writes=["rstd"])
                for h in range(4):
                    P.add("dve", lambda e, h=h: e.scalar_tensor_tensor(
                        cat[:, h * 128:(h + 1) * 128], b1[:, h * 128:(h + 1) * 128], rstd[:, h:h + 1],
                        oa2[:, h * 128:(h + 1) * 128], ALU.mult, ALU.mult), reads=["b1", "rstd", "gs"], writes=["catA"])
            else:
                for h in range(4):
                    P.add("dve", lambda e, h=h: e.bn_stats(bst[:, h, :], b1[:, h * 128:(h + 1) * 128]), reads=["b1"], writes=["bst"])
                for h in range(4):
                    P.add("dve", lambda e, h=h: e.bn_aggr(bmv[:, h, :], bst[:, h, :]), reads=["bst"], writes=["bmv"])
                P.add("pool", lambda e: e.tensor_scalar(vv[:, :], bmv[:, :, 1], EPS, None, ALU.add), reads=["bmv"], writes=["vv"])
                P.add("pool", lambda e: e.tensor_tensor(rstd[:, :], vv[:, :], c_("neghalf"), ALU.pow), reads=["vv", "cp"], writes=["rstd"])
                for h in range(4):
                    P.add("dve", lambda e, h=h: e.tensor_scalar(on[:, h * 128:(h + 1) * 128], b1[:, h * 128:(h + 1) * 128],
                                                                bmv[:, h, 0:1], rstd[:, h:h + 1], ALU.subtract, ALU.mult),
                          reads=["b1", "bmv", "rstd"], writes=["on"])
                P.add("dve", lambda e: e.tensor_tensor(cat[:, 0:512], on[:, :], oa2[:, :], ALU.mult), reads=["on", "gs"], writes=["catA"])

        def s_part(t, sl):
            if L == 0:
                P.add("act", lambda e: e.activation(squ[:, :], b7[:, :], AF.Square, scale=math.sqrt(GC1)), reads=["b7"], writes=["squ"])
                P.add("dve", lambda e: e.scalar_tensor_tensor(inn[:, :], squ[:, :], GC0, b7[:, :], ALU.add, ALU.mult),
                      reads=["squ", "b7"], writes=["inn"])
                P.add("act", lambda e: e.activation(sg1[:, :], inn[:, :], AF.Sigmoid, scale=2.0), reads=["inn"], writes=["sg1"])
                P.add("dve", lambda e: e.tensor_tensor(gu[:, :], b7[:, :], sg1[:, :], ALU.mult), reads=["b7", "sg1"], writes=["gu"])
                proj(sl, 2048, 512, b6[:, :], "b6")
                P.add("act", lambda e: e.activation(squ[:, :], b6[:, :], AF.Square, scale=math.sqrt(GC1)), reads=["b6"], writes=["squ"])
                P.add("dve", lambda e: e.scalar_tensor_tensor(inn[:, :], squ[:, :], GC0, b6[:, :], ALU.add, ALU.mult),
                      reads=["squ", "b6"], writes=["inn"])
                P.add("act", lambda e: e.activation(sg1[:, :], inn[:, :], AF.Sigmoid, scale=2.0), reads=["inn"], writes=["sg1"])
                P.add("dve", lambda e: e.tensor_tensor(gsv[:, :], b6[:, :], sg1[:, :], ALU.mult), reads=["b6", "sg1"], writes=["gsv"])
                for g in range(4):
                    P.add("dve", lambda e, g=g: e.bn_stats(bstS[:, g, :], gsv[:, g * 128:(g + 1) * 128]), reads=["gsv"], writes=["bstS"])
                for g in range(4):
                    P.add("dve", lambda e, g=g: e.bn_aggr(bmvS[:, g, :], bstS[:, g, :]), reads=["bstS"], writes=["bmvS"])
                P.add("pool", lambda e: e.tensor_scalar(vvS[:, :], bmvS[:, :, 1], EPS, None, ALU.add), reads=["bmvS"], writes=["vvS"])
                P.add("pool", lambda e: e.tensor_tensor(rstdS[:, :], vvS[:, :], c_("neghalf"), ALU.pow), reads=["vvS", "cp"], writes=["rstdS"])
                for g in range(4):
                    P.add("dve", lambda e, g=g: e.tensor_scalar(nrm[:, g * 128:(g + 1) * 128], gsv[:, g * 128:(g + 1) * 128],
                                                                bmvS[:, g, 0:1], rstdS[:, g:g + 1], ALU.subtract, ALU.mult),
                          reads=["gsv", "bmvS", "rstdS"], writes=["nrm"])
                P.add("dve", lambda e: e.tensor_tensor(nrm2[:, :], nrm[:, :], c_("sguG"), ALU.mult), reads=["nrm", "cp"], writes=["nrm2"])
                P.add("dve", lambda e: e.tensor_tensor(svn[:, :], nrm2[:, :], c_("sguB"), ALU.add), reads=["nrm2", "cp"], writes=["svn"])
                for g in range(4):
                    P.add("pe", lambda e, g=g: e.matmul(b2[:, g * 128:(g + 1) * 128], wsTb[:, g, :], svn[:, g * 128:(g + 1) * 128],
                                                        start=True, stop=True), reads=["wsTb", "svn"], writes=["b2"])
                for g in range(4):
                    P.add("dve", lambda e, g=g: e.scalar_tensor_tensor(
                        t1[:, g * 128:(g + 1) * 128], b2[:, g * 128:(g + 1) * 128], c_("bs", g, g + 1),
                        gu[:, g * 128:(g + 1) * 128], ALU.add, ALU.mult), reads=["b2", "cp", "gu"], writes=["t1"])
                proj(sl, 2560, 512, b7[:, :], "b7")
                P.add("act", lambda e: e.activation(sg1[:, :], b7[:, :], AF.Sigmoid), reads=["b7"], writes=["sg1"])
                P.add("dve", lambda e: e.tensor_tensor(sg2[:, :], b7[:, :], sg1[:, :], ALU.mult), reads=["b7", "sg1"], writes=["sg2"])
                P.add("dve", lambda e: e.tensor_tensor(cat[:, 512:1024], t1[:, :], sg2[:, :], ALU.mult), reads=["t1", "sg2"], writes=["catB"])
            else:
                cur = t % 2
                P.add("act", lambda e, cur=cur: e.copy(pbuf[cur][:, :], b7[:, :]), reads=["b7"], writes=[f"pb{cur}"])
                ti = 0 if t == 0 else 1
                for g in range(4):
                    P.add("pe", lambda e, g=g, cur=cur, ti=ti: e.matmul(b3[:, g * 128:(g + 1) * 128], pbuf[cur][:, g * 128:(g + 1) * 128],
                                                                        gcb[ti][:, g, :], start=True, stop=False),
                          reads=[f"pb{cur}", f"gcb{ti}"], writes=["b3"])
                    P.add("pe", lambda e, g=g, cur=cur, ti=ti: e.matmul(b3[:, g * 128:(g + 1) * 128], pbuf[1 - cur][:, g * 128:(g + 1) * 128],
                                                                        gpb[ti][:, g, :], start=False, stop=True),
                          reads=[f"pb{1 - cur}", f"gpb{ti}"], writes=["b3"])
                ik = "icn0" if t == 0 else "icn"
                P.add("dve", lambda e, ik=ik: e.tensor_tensor(plT[:, :, :], b3[:, :].rearrange("p (g t) -> p g t", g=4),
                                                              c_(ik).rearrange("p (g t) -> p g t", g=4), ALU.mult),
                      reads=["b3", "cp"], writes=["plT"])
                for g in range(4):
                    P.add("pe", lambda e, g=g: e.matmul(b2[:, g * 128:(g + 1) * 128], plT[:, g, :], wpb[:, g, :], start=True, stop=True),
                          reads=["plT", "wpb"], writes=["b2"])
                proj(sl, 2048, 512, b6[:, :], "b6")
                P.add("act", lambda e: e.activation(sg1[:, :], b6[:, :], (AF.Sigmoid if _os.environ.get("K_NOSILU") else AF.Silu)), reads=["b6"], writes=["sg1"])
                P.add("dve", lambda e: e.tensor_tensor(oa1[:, :], b2[:, :], c_("pscale"), ALU.mult), reads=["b2", "cp"], writes=["oa1"])
                P.add("dve", lambda e: e.tensor_tensor(cat[:, 512:1024], oa1[:, :], sg1[:, :], ALU.mult), reads=["oa1", "sg1"], writes=["catB"])

        def tail_part(t, sl):
            yk = ("b5", "tp")
            ybank = (b5, tp[:, :, :].rearrange("p a c -> p (a c)").bitcast(F32))
            for kc in range(8):
                P.add("pe", lambda e, kc=kc: e.transpose(tp[:, kc, :], cat[:, kc * 128:(kc + 1) * 128], identb[:, :]),
                      reads=["catA", "catB", "identb"], writes=["tp"])
            P.add("act", lambda e: e.copy(catT[:, 0:4, :], tp[:, 0:4, :]), reads=["tp"], writes=["catTa"])
            P.add("act", lambda e: e.copy(catT[:, 4:8, :], tp[:, 4:8, :]), reads=["tp"], writes=["catTb"])
            for nb in range(2):
                for kc in range(8):
                    P.add("pe", lambda e, kc=kc, nb=nb: e.matmul(ybank[nb][:, :], catT[:, kc, :], wo[:, kc, nb * 512:(nb + 1) * 512],
                                                                 start=(kc == 0), stop=(kc == 7)),
                          reads=["catTa" if kc < 4 else "catTb", "wo"], writes=[yk[nb]])
                P.add("dve", lambda e, nb=nb, sl=sl: e.scalar_tensor_tensor(
                    r32[:, nb * 512:(nb + 1) * 512], x32[sl][:, nb * 512:(nb + 1) * 512], ALPHA, ybank[nb][:, :], ALU.mult, ALU.add),
                    reads=[f"x32_{sl}", yk[nb]], writes=[f"r32_{nb}"])
                P.add("dve", lambda e, nb=nb: e.bn_stats(bst2[:, nb * 6:(nb + 1) * 6], r32[:, nb * 512:(nb + 1) * 512]), reads=[f"r32_{nb}"], writes=["bst2"])
            P.add("dve", lambda e: e.bn_aggr(bmv2[:, :], bst2[:, :]), reads=["bst2"], writes=["bmv2"])
            P.add("pool", lambda e: e.tensor_scalar(vv2[:, :], bmv2[:, 1:2], EPS, None, ALU.add), reads=["bmv2"], writes=["vv2"])
            P.add("pool", lambda e: e.tensor_tensor(rstd2[:, :], vv2[:, :], c_("neghalf", 0, 1), ALU.pow), reads=["vv2", "cp"], writes=["rstd2"])
            P.add("dve", lambda e: e.scalar_tensor_tensor(n32[:, :], r32[:, :], bmv2[:, 0:1], c_("lng"), ALU.subtract, ALU.mult),
                  reads=["r32_0", "r32_1", "bmv2", "cp"], writes=["n32"])
            P.add("dve", lambda e, sl=sl: e.scalar_tensor_tensor(o32[sl][:, :], n32[:, :], rstd2[:, 0:1], c_("lnb"), ALU.mult, ALU.add),
                  reads=["n32", "rstd2", "cp"], writes=[f"o32_{sl}"])
            P.add("sp", lambda e, sl=sl, t=t: e.dma_start(out=yout[t * 128:(t + 1) * 128, :], in_=o32[sl][:, :]),
                  reads=[f"o32_{sl}"], writes=[f"yout{sl}"] + ([f"{ykey}{t}"] if ykey else []), dma=f"os{sl}")

        if STOPL > 0:
            load_x(0)
        if _os.environ.get("K_NOTHREAD"):
            for t_ in range(NT_RUN):
                tile(t_)
        elif not full:
            if NT_RUN > 1:
                load_x(1)
            P.ops.extend(P.capture(lambda: tile(0, "H")))
            for t_ in range(NT_RUN):
                g_ = P.capture(lambda: tile(t_, "G"))
                hd_ = P.capture(lambda: tile(t_ + 1, "H")) if t_ + 1 < NT_RUN else []
                ngs_ = int(len(g_) * float(_os.environ.get("K_SGF", 1.0)))
                nhs_ = int(len(hd_) * float(_os.environ.get("K_SHF", 0.5)))
                P.ops.extend(Prog.merge(g_[:ngs_], hd_[:nhs_]))
                P.ops.extend(hd_[nhs_:])
                P.ops.extend(g_[ngs_:])
                if t_ + 2 < NT_RUN:
                    load_x(t_ + 2)
        else:
            if NT_RUN > 1:
                load_x(1)
            P.ops.extend(P.capture(lambda: tile(0, "H")))
            for t_ in range(NT_RUN):
                g_ = P.capture(lambda: tile(t_, "G"))
                s_ = P.capture(lambda: tile(t_, "S"))
                ng_ = int(len(g_) * float(_os.environ.get("K_GF", 1.0)))
                ns_ = int(len(s_) * float(_os.environ.get("K_SF", 0.85)))
                P.ops.extend(Prog.merge(g_[:ng_], s_[:ns_]))
                P.ops.extend(g_[ng_:])
                P.ops.extend(s_[ns_:])
                tl_ = P.capture(lambda: tile(t_, "T"))
                hd_ = P.capture(lambda: tile(t_ + 1, "H")) if t_ + 1 < NT_RUN else []
                nh_ = int(len(hd_) * float(_os.environ.get("K_TF", 0.2)))
                P.ops.extend(Prog.merge(tl_, hd_[:nh_]))
                P.ops.extend(hd_[nh_:])
                if t_ + 2 < NT_RUN:
                    load_x(t_ + 2)

        if full:
            P.add("sp", None, reads=["yout0", "yout1"])
        else:
            st = env["st_local"]
            P.add("sp", lambda e: e.dma_start(out=st[0:64, :].rearrange("p (a b) -> p a b", a=4), in_=S32[:, :, :]),
                  reads=["S32"], writes=["st_local"], dma="of")
            if L == 1:
                P.add("sp", lambda e: e.dma_start(out=st[64:192, :], in_=plst[:, :]), reads=["plst"], writes=["st_local"], dma="of")
            if _os.environ.get("K_NOCC"):
                P.add("sp", lambda e: e.dma_start(out=env["st_gath_t"].ap()[0:(64 if L == 0 else 192), :], in_=env["st_local_t"].ap()),
                      reads=["st_local"], writes=[env["st_gath_key"]], dma="of2")
            else:
              P.add("pool", lambda e: e.collective_compute(
                "AllGather", ALU.bypass, replica_groups=[[0, 1], [2, 3], [4, 5], [6, 7]],
                ins=[env["st_local_t"].ap().opt()], outs=[env["st_gath_t"].ap().opt()]),
                reads=["st_local"], writes=[env["st_gath_key"]], dma=f"cc{L}", inc=1)
        last = {}
        for op in P.ops:
            if op.fn is not None and op.dma is not None and not (op.dma.startswith("sg") or op.dma.startswith("xs")):
                last[op.dma] = op
        P.add("sp", None, xdeps=list(last.values()))
        P.emit(nc, b.es)
        b.pes = None


def build():
    b = B()
    nc, P = b.nc, b.P
    cp0_, offs0 = l0_pack(*[np.zeros(s_, np.float32) for s_ in ((16, 256), (256,), (4, 128), (4, 128), (4, 128), (4, 128, 128), (4, 128), (1024,), (1024,))])
    cp1_, offs1 = l1_pack(np.zeros((4, 128), np.float32), np.zeros((4, 128, 128), np.float32), np.zeros((512,), np.float32),
                          np.zeros((1024,), np.float32), np.zeros((1024,), np.float32), True)
    ncp0, ncp1 = cp0_.shape[1], cp1_.shape[1]
    with b.es:
        xin = b.dram("x", [TOK, D], F32, "ExternalInput").ap()
        cp0d = b.dram("cp0", [128, ncp0], F32, "ExternalInput").ap()
        cp1d = b.dram("cp1", [128, ncp1], F32, "ExternalInput").ap()
        w0d = b.dram("w0m", [D, 3072], F32, "ExternalInput").ap()
        wald = b.dram("w0a", [D, 16], F32, "ExternalInput").ap()
        wo0d = b.dram("wo0", [D, D], F32, "ExternalInput").ap()
        w1d = b.dram("w1m", [D, 2560], F32, "ExternalInput").ap()
        wo1d = b.dram("wo1", [D, D], F32, "ExternalInput").ap()
        posd = b.dram("pos", [128, NT], I32, "ExternalInput").ap()
        flgd = b.dram("flag", [128, 1], F32, "ExternalInput").ap()
        yout = b.dram("y", [TOK, D], F32, "ExternalOutput").ap()
        x1s = b.dram("x1s", [TOK, D], F32).ap()
        stA_t = b.dram("stA", [64, 512], F32)
        stG_t = b.dram("stG", [128, 512], F32)
        stB_t = b.dram("stB", [192, 512], F32)
        stH_t = b.dram("stH", [384, 512], F32)
        stA, stG, stB, stH = stA_t.ap(), stG_t.ap(), stB_t.ap(), stH_t.ap()

        psum = (b.ps("tp", [128, 8, 128], BF16),) + tuple(b.ps(f"b{i}", [128, 512]) for i in range(1, 8))
        identb = b.tsb("identb", [128, 128], BF16)
        identf = b.tsb("identf", [128, 128])
        flg = b.tsb("flg", [128, 1])
        posi = b.tsb("posi", [128, NT], I32)
        posf = b.tsb("posf", [128, NT])
        wmA1 = b.tsb("wmA1", [128, 8, 1024], BF16)
        P.add("sp", lambda e: e.dma_start(out=flg[:, :], in_=flgd[:, :]), writes=["flg"], dma="c4")
        P.add("sp", lambda e: e.dma_start(out=posi[:, :], in_=posd[:, :]), writes=["posi"], dma="c2")
        P.add("dve", lambda e: e.tensor_copy(posf[:, :], posi[:, :]), reads=["posi"], writes=["posf"])
        P.add("sp", lambda e: e.dma_start(out=identf[:, :], in_=cp0d[:, 0:128]), writes=["identf"], dma="c5")
        P.add("dve", lambda e: e.tensor_copy(identb[:, :], identf[:, :]), reads=["identf"], writes=["identb"])

        NSTG = 4
        stg = [b.tsb(f"stg{i}", [128, 512]) for i in range(NSTG)]
        b.bg = []
        b.stg_i = 0

        def wload(dst, dcol, src, scol, ncols, key, now=False):
            for kc in range(8):
                for c0 in range(0, ncols, 512):
                    n = min(512, ncols - c0)

                    def job(kc=kc, c0=c0, n=n):
                        i = b.stg_i % NSTG
                        b.stg_i += 1
                        P.add("sp", lambda e: e.dma_start(out=stg[i][:, 0:n], in_=src[kc * 128:(kc + 1) * 128, scol + c0:scol + c0 + n]),
                              writes=[f"stg{i}"], dma=f"sg{i}")
                        P.add("act", lambda e: e.copy(dst[:, kc, dcol + c0:dcol + c0 + n], stg[i][:, 0:n]),
                              reads=[f"stg{i}"], writes=[key])
                    if now:
                        job()
                    else:
                        b.bg.append(job)

        def bg_flush():
            while b.bg:
                b.bg.pop(0)()
        b.bg_flush = bg_flush

        with contextlib.ExitStack() as es0:
            cp0 = b.tsb("cp0", [128, ncp0], es=es0)
            wm0 = b.tsb("wm0", [128, 8, 3072], BF16, es=es0)
            wal = b.tsb("wal", [128, 8, 16], BF16, es=es0)
            wo0 = b.tsb("wo0", [128, 8, D], BF16, es=es0)
            P.add("sp", lambda e: e.dma_start(out=cp0[:, :], in_=cp0d[:, :]), writes=["cp"], dma="c0")
            wload(wm0, 0, w0d, 0, 1024, "wm0A", now=True)
            wload(wal, 0, wald, 0, 16, "wal", now=True)
            wload(wm0, 1024, w0d, 1024, 2048, "wm0B")
            wload(wo0, 0, wo0d, 0, 1024, "wo")
            wload(wmA1, 0, w1d, 0, 1024, "wm1A")
            b.cp, b.offs = cp0, offs0

            def wmf0(kc, off, n):
                return wm0[:, kc, off:off + n], ("wm0A" if off < 1024 else "wm0B")
            env = dict(xin=xin, wm=wmf0, identb=identb, wal=wal, wo=wo0, flg=flg, psum=psum, xkey=None,
                       st_local=stA, st_local_t=stA_t, st_gath_t=stG_t, st_gath_key="stG")
            NPH = int(_os.environ.get("K_PH", 4))
            if not _os.environ.get("K_SKIP0S"):
                phase(b, "L0S", env)
                P.barrier()
            env.update(yout=x1s, ykey="x1s", sinit=stG[0:64, :], sinit_key="stG")
            bg_flush()
            if NPH >= 2 and not _os.environ.get("K_SKIP0F"):
                phase(b, "L0F", env)
                P.barrier()

        with contextlib.ExitStack() as es1:
            cp1 = b.tsb("cp1", [128, ncp1], es=es1)
            wmB1 = b.tsb("wmB1", [128, 8, 1536], BF16, es=es1)
            wo1 = b.tsb("wo1", [128, 8, D], BF16, es=es1)
            P.add("sp", lambda e: e.dma_start(out=cp1[:, :], in_=cp1d[:, :]), writes=["cp"], dma="c6")
            wload(wmB1, 0, w1d, 1024, 1536, "wm1B")
            wload(wo1, 0, wo1d, 0, 1024, "wo")
            b.cp, b.offs = cp1, offs1

            def wmf1(kc, off, n):
                if off < 1024:
                    return wmA1[:, kc, off:off + n], "wm1A"
                return wmB1[:, kc, off - 1024:off - 1024 + n], "wm1B"
            env = dict(xin=x1s, wm=wmf1, identb=identb, wo=wo1, flg=flg, psum=psum, xkey="x1s", posf=posf,
                       st_local=stB, st_local_t=stB_t, st_gath_t=stH_t, st_gath_key="stH")
            if NPH >= 3 and not _os.environ.get("K_SKIP1S"):
                phase(b, "L1S", env)
                P.barrier()
            env.update(yout=yout, ykey=None, sinit=stH[0:64, :], pprev=stH[64:192, :], sinit_key="stH")
            bg_flush()
            if NPH >= 4:
                phase(b, "L1F", env)
        if P.ops:
            P.emit(nc, b.es)
    return nc, offs0, offs1


_CACHE = {}


def _get():
    if "nc" not in _CACHE:
        _CACHE["nc"] = build()
    return _CACHE["nc"]


def kernel(x, positions, l0_w_in, l0_w_a2, l0_b_a, l0_gla_norm_g, l0_sgu_ln_g, l0_sgu_ln_b,
           l0_w_s, l0_b_s, l0_w_out, l0_ln_g, l0_ln_b, l1_w_in, l1_ret_norm_g, l1_w_pool,
           l1_pool_scale, l1_w_out, l1_ln_g, l1_ln_b):
    f = lambda a: np.ascontiguousarray(np.asarray(a), dtype=np.float32)
    x = f(x)
    positions = np.ascontiguousarray(np.asarray(positions), dtype=np.int32)
    w0 = f(l0_w_in)
    w0m = np.ascontiguousarray(np.concatenate(
        [w0[:, 0:256], w0[:, 256:512], w0[:, 512:1024], w0[:, 1024:1536], w0[:, 1552:2064], w0[:, 2064:2576], w0[:, 2576:3088]], axis=1))
    w0a = np.ascontiguousarray(w0[:, 1536:1552])
    w1m = f(l1_w_in)
    wo0, wo1 = f(l0_w_out), f(l1_w_out)
    cp0, offs0 = l0_pack(f(l0_w_a2), f(l0_b_a), f(l0_gla_norm_g), f(l0_sgu_ln_g), f(l0_sgu_ln_b), f(l0_w_s), f(l0_b_s),
                         f(l0_ln_g), f(l0_ln_b))
    cp1 = []
    for hf in range(2):
        c_, offs1 = l1_pack(f(l1_ret_norm_g), f(l1_w_pool), f(l1_pool_scale), f(l1_ln_g), f(l1_ln_b), hf == 0)
        cp1.append(c_)
    nc, o0, o1 = _get()
    assert o0 == offs0 and o1 == offs1
    in_maps = []
    for c in range(NCORES):
        bi, hf = c // 2, c % 2
        in_maps.append({
            "x": np.ascontiguousarray(x[bi, hf * TOK:(hf + 1) * TOK, :]),
            "cp0": cp0, "cp1": cp1[hf], "w0m": w0m, "w0a": w0a, "wo0": wo0, "w1m": w1m, "wo1": wo1,
            "pos": np.ascontiguousarray(positions[bi, hf * TOK:(hf + 1) * TOK].reshape(NT, 128).T),
            "flag": np.full((128, 1), float(hf), np.float32),
        })
    res = run_bass_kernel_spmd(nc, in_maps, core_ids=list(range(NCORES)))
    out = np.empty((4, 2 * TOK, D), np.float32)
    for c in range(NCORES):
        out[c // 2, (c % 2) * TOK:(c % 2 + 1) * TOK, :] = np.asarray(res.results[c]["y"], np.float32)
    return out
```
